# Optimizing a Trainium2 kernel written in Bass

```python
import math
import jax
import jax.numpy as jnp
from jax import lax
import numpy as np

D_MODEL = 1024
BATCH = 4
SEQ = 8192
DEPTH = 4

GRID_W = 64
CTX_LEN = 256
ROPE_BASE = 10000.0
N_MOD = 6

MLA_HEADS = 8
MLA_Q_RANK = 256
MLA_KV_RANK = 128
MLA_NOPE = 64
MLA_ROPE = 32
MLA_V = 64
MLA_SCALE = (MLA_NOPE + MLA_ROPE) ** -0.5
Q_BLOCK = 128

RET_HEADS = 8
RET_DK = 64
RET_DV = 64
RET_CHUNK = 128

DN_HEADS = 8
DN_DK = 128
DN_DV = 128
DN_CONV = 5
DN_CHUNK = 64

PEER_HEADS = 8
PEER_KEYS = 128
PEER_N = PEER_KEYS * PEER_KEYS
PEER_QDIM = 256
PEER_TOPK = 16
PEER_BLOCK = 128

DEEPNORM_ALPHA = (2 * DEPTH) ** 0.25
DEEPNORM_BETA = (8 * DEPTH) ** -0.25
N_EVEN = (DEPTH + 1) // 2
N_ODD = DEPTH // 2

AR_SIZES = (MLA_Q_RANK, MLA_KV_RANK, MLA_ROPE, RET_HEADS * RET_DK, RET_HEADS * RET_DK, RET_HEADS * RET_DV, RET_HEADS * RET_DV)
AR_IN = MLA_Q_RANK + MLA_KV_RANK + MLA_ROPE + 2 * RET_HEADS * RET_DK + 2 * RET_HEADS * RET_DV
AR_OUT = MLA_HEADS * MLA_V + RET_HEADS * RET_DV
DN_SIZES = (DN_HEADS * DN_DK, DN_HEADS * DN_DK, DN_HEADS * DN_DV, DN_HEADS * DN_DV, 2 * DN_HEADS, 2 * DN_HEADS)
DN_IN = 2 * DN_HEADS * DN_DK + 2 * DN_HEADS * DN_DV + 4 * DN_HEADS
DN_CONV_CH = 2 * DN_HEADS * DN_DK + DN_HEADS * DN_DV
DN_OUT = DN_HEADS * DN_DV

kernel_name = 'hybrid_mla_retention_gdn_peer_dit'

F32 = jnp.float32


def split_cols(z, sizes):
    return jnp.split(z, [int(i) for i in np.cumsum(sizes)[:-1]], axis=-1)


def layer_norm(x, g, b, eps=1e-5):
    xf = x.astype(F32)
    xc = xf - jnp.mean(xf, -1, keepdims=True)
    var = jnp.mean(xc * xc, -1, keepdims=True)
    return (xc * lax.rsqrt(var + eps) * g + b).astype(x.dtype)


def rms_norm(x, g, eps=1e-6):
    xf = x.astype(F32)
    return (xf * lax.rsqrt(jnp.mean(xf * xf, -1, keepdims=True) + eps) * g).astype(x.dtype)


def l2_norm(x, eps=1e-6):
    xf = x.astype(F32)
    return (xf * lax.rsqrt(jnp.sum(xf * xf, -1, keepdims=True) + eps)).astype(x.dtype)


def group_norm_heads(o, g, eps=1e-5):
    of = o.astype(F32)
    oc = of - jnp.mean(of, -1, keepdims=True)
    y = oc * lax.rsqrt(jnp.mean(oc * oc, -1, keepdims=True) + eps)
    b, h, L, dv = o.shape
    return (jnp.transpose(y, (0, 2, 1, 3)).reshape(b, L, h * dv) * g).astype(o.dtype)


def modulate(x, shift, scale):
    return x * (1 + scale) + shift


def flip_seq(t):
    return jnp.flip(t, axis=2)


def axial_rope(length, dim):
    rows = length // GRID_W
    n_freq = dim // 4
    inv = ROPE_BASE ** (-jnp.arange(n_freq, dtype=F32) / n_freq)
    row = jnp.repeat(jnp.arange(rows, dtype=F32), GRID_W)
    col = jnp.tile(jnp.arange(GRID_W, dtype=F32), rows)
    ang = jnp.concatenate([row[:, None] * inv, col[:, None] * inv], axis=-1)
    return jnp.cos(ang), jnp.sin(ang)


def apply_rope(x, cos, sin):
    x1, x2 = jnp.split(x, 2, axis=-1)
    return jnp.concatenate([x1 * cos - x2 * sin, x2 * cos + x1 * sin], axis=-1).astype(x.dtype)


def mla_heads(cq, ckv, kr, q_norm, w_uq, kv_norm, w_ukv, rope):
    b, L, _ = cq.shape
    q = (rms_norm(cq, q_norm) @ w_uq).reshape(b, L, MLA_HEADS, MLA_NOPE + MLA_ROPE)
    kv = (rms_norm(ckv, kv_norm) @ w_ukv).reshape(b, L, MLA_HEADS, MLA_NOPE + MLA_V)
    qn, qr = q[..., :MLA_NOPE], q[..., MLA_NOPE:]
    kn, v = kv[..., :MLA_NOPE], kv[..., MLA_NOPE:]
    if rope is not None:
        cos, sin = rope
        qr = apply_rope(qr, cos[:, None], sin[:, None])
        kr = apply_rope(kr, cos, sin)
    return qn, qr, kn, kr, v


def softmax_attend(qn, qr, kn, kr, v):
    s = jnp.einsum('bqhd,bkhd->bhqk', qn, kn) + jnp.einsum('bqhr,bkr->bhqk', qr, kr)
    p = jax.nn.softmax(s.astype(F32) * MLA_SCALE, axis=-1)
    return jnp.einsum('bhqk,bkhd->bqhd', p.astype(v.dtype), v)


def latent_attention(qn, qr, kn, kr, v):
    b, L, h, _ = qn.shape
    nb = L // Q_BLOCK
    def blocks(t):
        return jnp.swapaxes(t.reshape((b, nb, Q_BLOCK) + t.shape[2:]), 0, 1)
    o = lax.map(lambda qs: softmax_attend(qs[0], qs[1], kn, kr, v), (blocks(qn), blocks(qr)))
    return jnp.swapaxes(o, 0, 1).reshape(b, L, h * MLA_V)


def retention_dir(q, k, v, log_gamma, state0, strict):
    b, h, L, dk = q.shape
    dv = v.shape[-1]
    C = RET_CHUNK
    n = L // C
    qc = q.astype(F32).reshape(b, h, n, C, dk)
    kc = k.astype(F32).reshape(b, h, n, C, dk)
    vc = v.astype(F32).reshape(b, h, n, C, dv)
    pos = jnp.arange(C, dtype=F32)
    dist = pos[:, None] - pos[None, :]
    mask = (dist > 0) if strict else (dist >= 0)
    dmat = jnp.where(mask, jnp.exp(jnp.where(mask, dist, 0.0) * log_gamma[:, None, None]), 0.0)
    scores = jnp.einsum('bhncd,bhnsd->bhncs', qc, kc) * dmat[:, None]
    intra = jnp.einsum('bhncs,bhnsv->bhncv', scores, vc)
    q_dec = jnp.exp((pos + 1.0) * log_gamma[:, None])
    k_dec = jnp.exp((C - 1.0 - pos) * log_gamma[:, None])
    chunk_dec = jnp.exp(C * log_gamma)[None, :, None, None]
    kv = jnp.einsum('bhncd,hc,bhncv->nbhdv', kc, k_dec, vc)
    def step(s, kv_n):
        return s * chunk_dec + kv_n, s
    s_final, s_start = lax.scan(step, state0, kv)
    inter = jnp.einsum('bhncd,hc,nbhdv->bhncv', qc, q_dec, s_start)
    return (intra + inter).reshape(b, h, L, dv), s_final


def ret_heads(rq, rk, rv, rope):
    b, L, _ = rq.shape
    q = rq.reshape(b, L, RET_HEADS, RET_DK)
    k = rk.reshape(b, L, RET_HEADS, RET_DK)
    if rope is not None:
        cos, sin = rope
        q = apply_rope(q, cos[:, None], sin[:, None])
        k = apply_rope(k, cos[:, None], sin[:, None])
    k = k * RET_DK ** -0.5
    v = rv.reshape(b, L, RET_HEADS, RET_DV)
    return jnp.transpose(q, (0, 2, 1, 3)), jnp.transpose(k, (0, 2, 1, 3)), jnp.transpose(v, (0, 2, 1, 3))


def attn_retention_mixer(hl, hc, w_in, q_norm, w_uq, kv_norm, w_ukv, gn_g, w_out, rope_mla, rope_ret, ctx_out):
    zl = split_cols(hl @ w_in, AR_SIZES)
    zc = split_cols(hc @ w_in, AR_SIZES)
    qn_l, qr_l, kn_l, kr_l, v_l = mla_heads(zl[0], zl[1], zl[2], q_norm, w_uq, kv_norm, w_ukv, rope_mla)
    qn_c, qr_c, kn_c, kr_c, v_c = mla_heads(zc[0], zc[1], zc[2], q_norm, w_uq, kv_norm, w_ukv, None)
    kn_all = jnp.concatenate([kn_c, kn_l], axis=1)
    kr_all = jnp.concatenate([kr_c, kr_l], axis=1)
    v_all = jnp.concatenate([v_c, v_l], axis=1)
    mla_l = latent_attention(qn_l, qr_l, kn_all, kr_all, v_all)
    log_gamma = jnp.log1p(-jnp.exp2(-5.0 - jnp.arange(RET_HEADS, dtype=F32)))
    ql, kl, vl = ret_heads(zl[3], zl[4], zl[5], rope_ret)
    qc, kc, vc = ret_heads(zc[3], zc[4], zc[5], None)
    zero = jnp.zeros((hl.shape[0], RET_HEADS, RET_DK, RET_DV), F32)
    oc_f, s_f = retention_dir(qc, kc, vc, log_gamma, zero, False)
    oc_b, s_b = retention_dir(flip_seq(qc), flip_seq(kc), flip_seq(vc), log_gamma, zero, True)
    ol_f, _ = retention_dir(ql, kl, vl, log_gamma, s_f, False)
    ol_b, _ = retention_dir(flip_seq(ql), flip_seq(kl), flip_seq(vl), log_gamma, s_b, True)
    ret_l = group_norm_heads(ol_f + flip_seq(ol_b), gn_g) * jax.nn.silu(zl[6])
    yl = jnp.concatenate([mla_l, ret_l], axis=-1) @ w_out
    if not ctx_out:
        return yl, None
    b, Lc = hc.shape[0], hc.shape[1]
    mla_c = softmax_attend(qn_c, qr_c, kn_c, kr_c, v_c).reshape(b, Lc, MLA_HEADS * MLA_V)
    ret_c = group_norm_heads(oc_f + flip_seq(oc_b), gn_g) * jax.nn.silu(zc[6])
    yc = jnp.concatenate([mla_c, ret_c], axis=-1) @ w_out
    return yl, yc


def gated_delta_dir(q, k, v, g, beta, state0):
    b, h, L, dk = q.shape
    dv = v.shape[-1]
    C = DN_CHUNK
    n = L // C
    qc = q.astype(F32).reshape(b, h, n, C, dk)
    kc = k.astype(F32).reshape(b, h, n, C, dk)
    vc = v.astype(F32).reshape(b, h, n, C, dv)
    gc = jnp.cumsum(g.astype(F32).reshape(b, h, n, C), axis=-1)
    bc = beta.astype(F32).reshape(b, h, n, C, 1)
    pos = jnp.arange(C)
    incl = pos[:, None] >= pos[None, :]
    strict = pos[:, None] > pos[None, :]
    gdiff = gc[..., :, None] - gc[..., None, :]
    decay = jnp.where(incl, jnp.exp(jnp.where(incl, gdiff, 0.0)), 0.0)
    kb = kc * bc
    lmat = jnp.where(strict, jnp.einsum('bhncd,bhnsd->bhncs', kb, kc) * decay, 0.0)
    eye = jnp.eye(C, dtype=F32)
    t_inv = lax.linalg.triangular_solve(eye + lmat, jnp.broadcast_to(eye, lmat.shape), left_side=True, lower=True, unit_diagonal=True)
    u = jnp.einsum('bhncs,bhnsv->bhncv', t_inv, vc * bc)
    w = jnp.einsum('bhncs,bhnsd->bhncd', t_inv, kb * jnp.exp(gc)[..., None])
    attn = jnp.einsum('bhncd,bhnsd->bhncs', qc, kc) * decay
    q_dec = qc * jnp.exp(gc)[..., None]
    k_dec = kc * jnp.exp(gc[..., -1:] - gc)[..., None]
    chunk_dec = jnp.exp(gc[..., -1])
    xs = tuple(jnp.moveaxis(t, 2, 0) for t in (attn, q_dec, k_dec, u, w, chunk_dec))
    def step(s, inp):
        a_n, qd_n, kd_n, u_n, w_n, cd_n = inp
        v_new = u_n - jnp.einsum('bhcd,bhdv->bhcv', w_n, s)
        o_n = jnp.einsum('bhcd,bhdv->bhcv', qd_n, s) + jnp.einsum('bhcs,bhsv->bhcv', a_n, v_new)
        s = s * cd_n[..., None, None] + jnp.einsum('bhcd,bhcv->bhdv', kd_n, v_new)
        return s, o_n
    s_final, o = lax.scan(step, state0, xs)
    return jnp.moveaxis(o, 0, 2).reshape(b, h, L, dv), s_final


def short_conv(x, w):
    K = w.shape[0]
    return lax.conv_general_dilated(x, w.astype(x.dtype)[:, None, :], window_strides=(1,), padding=[((K - 1) // 2, K // 2)], dimension_numbers=('NWC', 'WIO', 'NWC'), feature_group_count=x.shape[-1])


def dn_heads(z, conv_w, a_log, dt_bias):
    b, L, _ = z.shape
    q, k, v, gate, a, bt = split_cols(z, DN_SIZES)
    qkv = jax.nn.silu(short_conv(jnp.concatenate([q, k, v], axis=-1), conv_w))
    q, k, v = split_cols(qkv, DN_SIZES[:3])
    q = l2_norm(q.reshape(b, L, DN_HEADS, DN_DK)) * DN_DK ** -0.5
    k = l2_norm(k.reshape(b, L, DN_HEADS, DN_DK))
    v = v.reshape(b, L, DN_HEADS, DN_DV)
    a = a.reshape(b, L, 2, DN_HEADS).astype(F32)
    g = -jnp.exp(a_log.astype(F32)) * jax.nn.softplus(a + dt_bias.astype(F32))
    beta = jax.nn.sigmoid(bt.reshape(b, L, 2, DN_HEADS).astype(F32))
    tr = lambda t: jnp.transpose(t, (0, 2, 1, 3))
    return tr(q), tr(k), tr(v), jnp.transpose(g, (2, 0, 3, 1)), jnp.transpose(beta, (2, 0, 3, 1)), gate


def gated_out(o, gate, norm_g):
    b, h, L, dv = o.shape
    y = rms_norm(jnp.transpose(o, (0, 2, 1, 3)), norm_g) * jax.nn.silu(gate.reshape(b, L, h, dv))
    return y.reshape(b, L, h * dv)


def deltanet_mixer(hl, hc, w_in, conv_w, a_log, dt_bias, norm_g, w_out, ctx_out):
    ql, kl, vl, gl, bl, zl = dn_heads(hl @ w_in, conv_w, a_log, dt_bias)
    qc, kc, vc, gc, bc, zc = dn_heads(hc @ w_in, conv_w, a_log, dt_bias)
    zero = jnp.zeros((hl.shape[0], DN_HEADS, DN_DK, DN_DV), F32)
    oc_f, s_f = gated_delta_dir(qc, kc, vc, gc[0], bc[0], zero)
    oc_b, s_b = gated_delta_dir(flip_seq(qc), flip_seq(kc), flip_seq(vc), flip_seq(gc[1]), flip_seq(bc[1]), zero)
    ol_f, _ = gated_delta_dir(ql, kl, vl, gl[0], bl[0], s_f)
    ol_b, _ = gated_delta_dir(flip_seq(ql), flip_seq(kl), flip_seq(vl), flip_seq(gl[1]), flip_seq(bl[1]), s_b)
    yl = gated_out(ol_f + flip_seq(ol_b), zl, norm_g) @ w_out
    if not ctx_out:
        return yl, None
    yc = gated_out(oc_f + flip_seq(oc_b), zc, norm_g) @ w_out
    return yl, yc


def peer(h, w_q, k1, k2, u_tab, v_tab):
    b, L, d = h.shape
    blocks = h.reshape(b * L // PEER_BLOCK, PEER_BLOCK, d)
    half = PEER_QDIM // 2
    def block(xb):
        q = (xb @ w_q).reshape(PEER_BLOCK, PEER_HEADS, 2, half)
        s1 = jnp.einsum('phd,hnd->phn', q[:, :, 0], k1)
        s2 = jnp.einsum('phd,hnd->phn', q[:, :, 1], k2)
        v1, i1 = lax.top_k(s1, PEER_TOPK)
        v2, i2 = lax.top_k(s2, PEER_TOPK)
        cand = (v1[..., :, None] + v2[..., None, :]).reshape(PEER_BLOCK, PEER_HEADS, PEER_TOPK * PEER_TOPK)
        cand_idx = (i1[..., :, None] * PEER_KEYS + i2[..., None, :]).reshape(PEER_BLOCK, PEER_HEADS, PEER_TOPK * PEER_TOPK)
        sc, j = lax.top_k(cand, PEER_TOPK)
        e_idx = jnp.take_along_axis(cand_idx, j, axis=-1)
        gate = jax.nn.softmax(sc.astype(F32), axis=-1)
        act = jax.nn.gelu(jnp.einsum('phkd,pd->phk', u_tab[e_idx], xb).astype(F32), approximate=False)
        return jnp.einsum('phk,phkd->pd', (gate * act).astype(xb.dtype), v_tab[e_idx])
    return lax.map(block, blocks).reshape(b, L, d)


def setup_inputs(seed: int = 0) -> dict:
    key = jax.random.key(seed)
    ks = iter(jax.random.split(key, 32))
    D = D_MODEL
    def nrm(shape, s):
        return jax.random.normal(next(ks), shape, F32) * s
    dt = jnp.exp(jax.random.uniform(next(ks), (N_ODD, 2, DN_HEADS), F32, minval=math.log(1e-3), maxval=math.log(1e-1)))
    return {
        'x': nrm((BATCH, SEQ, D), 1.0),
        'c': nrm((BATCH, D), 1.0),
        'ctx': nrm((BATCH, CTX_LEN, D), 1.0),
        'c_ctx': nrm((D,), 1.0),
        'ada_w': nrm((DEPTH, D, N_MOD * D), 0.5 * D ** -0.5),
        'ada_b': nrm((DEPTH, N_MOD * D), 0.02),
        'ln1_g': 1.0 + nrm((DEPTH, D), 0.02),
        'ln1_b': nrm((DEPTH, D), 0.02),
        'ln2_g': 1.0 + nrm((DEPTH, D), 0.02),
        'ln2_b': nrm((DEPTH, D), 0.02),
        'ar_w_in': nrm((N_EVEN, D, AR_IN), D ** -0.5),
        'mla_q_norm': 1.0 + nrm((N_EVEN, MLA_Q_RANK), 0.02),
        'mla_w_uq': nrm((N_EVEN, MLA_Q_RANK, MLA_HEADS * (MLA_NOPE + MLA_ROPE)), MLA_Q_RANK ** -0.5),
        'mla_kv_norm': 1.0 + nrm((N_EVEN, MLA_KV_RANK), 0.02),
        'mla_w_ukv': nrm((N_EVEN, MLA_KV_RANK, MLA_HEADS * (MLA_NOPE + MLA_V)), MLA_KV_RANK ** -0.5),
        'ret_gn_g': 1.0 + nrm((N_EVEN, RET_HEADS * RET_DV), 0.02),
        'ar_w_out': nrm((N_EVEN, AR_OUT, D), AR_OUT ** -0.5 * DEEPNORM_BETA),
        'dn_w_in': nrm((N_ODD, D, DN_IN), D ** -0.5),
        'dn_conv': nrm((N_ODD, DN_CONV, DN_CONV_CH), DN_CONV ** -0.5),
        'dn_a_log': jnp.log(jax.random.uniform(next(ks), (N_ODD, 2, DN_HEADS), F32, minval=1.0, maxval=16.0)),
        'dn_dt_bias': dt + jnp.log(-jnp.expm1(-dt)),
        'dn_norm_g': 1.0 + nrm((N_ODD, DN_DV), 0.02),
        'dn_w_out': nrm((N_ODD, DN_OUT, D), DN_OUT ** -0.5 * DEEPNORM_BETA),
        'peer_w_q': nrm((DEPTH, D, PEER_HEADS * PEER_QDIM), D ** -0.5),
        'peer_k1': nrm((DEPTH, PEER_HEADS, PEER_KEYS, PEER_QDIM // 2), (PEER_QDIM // 2) ** -0.5),
        'peer_k2': nrm((DEPTH, PEER_HEADS, PEER_KEYS, PEER_QDIM // 2), (PEER_QDIM // 2) ** -0.5),
        'peer_u': nrm((DEPTH, PEER_N, D), D ** -0.5),
        'peer_v': nrm((DEPTH, PEER_N, D), DEEPNORM_BETA * PEER_HEADS ** -0.5),
    }


def reference(x, c, ctx, c_ctx, ada_w, ada_b, ln1_g, ln1_b, ln2_g, ln2_b,
              ar_w_in, mla_q_norm, mla_w_uq, mla_kv_norm, mla_w_ukv, ret_gn_g, ar_w_out,
              dn_w_in, dn_conv, dn_a_log, dn_dt_bias, dn_norm_g, dn_w_out,
              peer_w_q, peer_k1, peer_k2, peer_u, peer_v):
    L = x.shape[1]
    rope_mla = axial_rope(L, MLA_ROPE)
    rope_ret = axial_rope(L, RET_DK)
    xl, xc = x, ctx
    for l in range(DEPTH):
        last = l == DEPTH - 1
        j = l // 2
        mod_l = (jax.nn.silu(c) @ ada_w[l] + ada_b[l])[:, None, :]
        mod_c = (jax.nn.silu(c_ctx) @ ada_w[l] + ada_b[l])[None, None, :]
        sh1l, sc1l, g1l, sh2l, sc2l, g2l = jnp.split(mod_l, N_MOD, axis=-1)
        sh1c, sc1c, g1c, sh2c, sc2c, g2c = jnp.split(mod_c, N_MOD, axis=-1)
        hl = modulate(xl, sh1l, sc1l)
        hc = modulate(xc, sh1c, sc1c)
        if l % 2 == 0:
            yl, yc = attn_retention_mixer(hl, hc, ar_w_in[j], mla_q_norm[j], mla_w_uq[j], mla_kv_norm[j], mla_w_ukv[j], ret_gn_g[j], ar_w_out[j], rope_mla, rope_ret, not last)
        else:
            yl, yc = deltanet_mixer(hl, hc, dn_w_in[j], dn_conv[j], dn_a_log[j], dn_dt_bias[j], dn_norm_g[j], dn_w_out[j], not last)
        xl = layer_norm(DEEPNORM_ALPHA * xl + g1l * yl, ln1_g[l], ln1_b[l])
        fl = peer(modulate(xl, sh2l, sc2l), peer_w_q[l], peer_k1[l], peer_k2[l], peer_u[l], peer_v[l])
        xl = layer_norm(DEEPNORM_ALPHA * xl + g2l * fl, ln2_g[l], ln2_b[l])
        if not last:
            xc = layer_norm(DEEPNORM_ALPHA * xc + g1c * yc, ln1_g[l], ln1_b[l])
            fc = peer(modulate(xc, sh2c, sc2c), peer_w_q[l], peer_k1[l], peer_k2[l], peer_u[l], peer_v[l])
            xc = layer_norm(DEEPNORM_ALPHA * xc + g2c * fc, ln2_g[l], ln2_b[l])
    return xl
```

```python
import math
from contextlib import ExitStack
import numpy as np
import concourse.bass as bass
import concourse.mybir as mybir
from concourse.bass_utils import run_bass_kernel_spmd

F32 = mybir.dt.float32
I32 = mybir.dt.int32
U32 = mybir.dt.uint32
AF = mybir.ActivationFunctionType
ALU = mybir.AluOpType
AX = mybir.AxisListType

NCORES = 8
D = 1024
DEPTH = 4
ALPHA = (2 * DEPTH) ** 0.25


class Buf:
    __slots__ = ("name", "w", "r", "dsem", "excl")

    def __init__(self, name="b", excl=False):
        self.name = name
        self.w = None
        self.r = []
        self.dsem = None
        self.excl = excl


class KB:
    ENG = ("pe", "act", "dve", "pool", "sp")

    def __init__(self, nc, stack):
        self.nc = nc
        self.stack = stack
        self.eng = {"pe": nc.tensor, "act": nc.scalar, "dve": nc.vector,
                    "pool": nc.gpsimd, "sp": nc.sync}
        self.sem = {}
        self.cnt = {}
        for e in self.ENG:
            self.sem[e] = stack.enter_context(nc.semaphore("s_" + e))
            self.cnt[e] = 0
        self.waited = {e: {} for e in self.ENG}
        self.semobj = dict(self.sem)
        self.dmacnt = {}
        self.nd = 0
        self.ninst = 0
        self.nt = 0

    def sb(self, shape, dt=F32, name=None):
        self.nt += 1
        t = self.stack.enter_context(self.nc.sbuf_tensor(name or ("t%d" % self.nt), list(shape), dt))
        return t, Buf(name or ("t%d" % self.nt))

    def ps(self, shape, dt=F32, name=None):
        self.nt += 1
        t = self.stack.enter_context(self.nc.psum_tensor(name or ("p%d" % self.nt), list(shape), dt))
        return t, Buf(name or ("p%d" % self.nt), excl=True)

    def _dsem(self, b):
        if b.dsem is None:
            key = "d%d" % self.nd
            self.nd += 1
            h = self.stack.enter_context(self.nc.semaphore(key))
            self.semobj[key] = h
            self.dmacnt[key] = 0
            b.dsem = key
        return b.dsem

    def _need(self, e, deps):
        eng = self.eng[e]
        best = {}
        for d in deps:
            if d is None:
                continue
            k, v = d
            if k in self.dmacnt:
                v = self.dmacnt[k]
            if e == "pe" and k == "pe":
                continue
            if best.get(k, 0) < v:
                best[k] = v
        for k, v in best.items():
            if self.waited[e].get(k, 0) < v:
                eng.wait_ge(self.semobj[k], v)
                self.waited[e][k] = v
                self.ninst += 1

    @staticmethod
    def _deps(reads, writes):
        deps = []
        for b in reads:
            deps.append(b.w)
        for b in writes:
            deps.append(b.w)
            deps.extend(b.r)
        return deps

    def op(self, e, fn, reads=(), writes=()):
        rx = [b for b in reads if b.excl]
        if rx:
            writes = list(writes) + rx
        self._need(e, self._deps(reads, writes))
        inst = fn()
        self.cnt[e] += 1
        inst.then_inc(self.sem[e], 1)
        tok = (e, self.cnt[e])
        for b in reads:
            b.r.append(tok)
        for b in writes:
            b.w = tok
            b.r = []
        self.ninst += 1
        return inst

    def dma(self, q, out, in_, reads=(), writes=(), fn=None, nowaw=False, **kw):
        self._need(q, self._deps(reads, () if nowaw else writes))
        key = self._dsem(writes[0])
        if fn is None:
            inst = self.eng[q].dma_start(out=out, in_=in_, **kw)
        else:
            inst = fn()
        inst.then_inc(self.semobj[key], 16)
        self.dmacnt[key] += 16
        tok = (key, self.dmacnt[key])
        for b in reads:
            b.r.append(tok)
        for b in writes:
            b.w = tok
            b.r = []
        self.ninst += 1
        return inst

    def finish(self, bufs):
        allc = [(e, self.cnt[e]) for e in self.ENG if self.cnt[e] > 0]
        for e in ("sp", "pool", "act"):
            self._need(e, [b.w for b in bufs] + [c for c in allc if c[0] != e])

    def mm(self, out, lhsT, rhs, start, stop, reads, writes):
        nc = self.nc
        return self.op("pe", lambda: nc.tensor.matmul(out, lhsT, rhs, start=start, stop=stop),
                       reads=reads, writes=writes)

    def tr(self, out, in_, ident, reads, writes):
        nc = self.nc
        return self.op("pe", lambda: nc.tensor.transpose(out, in_, ident), reads=reads, writes=writes)

    def act(self, out, in_, func, reads, writes, e="act", **kw):
        nc = self.nc
        return self.op("act", lambda: nc.scalar.activation(out=out, in_=in_, func=func, **kw),
                       reads=reads, writes=writes)

    def tt(self, out, in0, in1, op, reads, writes, e="dve"):
        eng = self.eng[e]
        return self.op(e, lambda: eng.tensor_tensor(out=out, in0=in0, in1=in1, op=op),
                       reads=reads, writes=writes)

    def ts(self, out, in0, s1, s2, op0, op1=None, reads=(), writes=(), e="dve", **kw):
        eng = self.eng[e]
        if op1 is None:
            return self.op(e, lambda: eng.tensor_scalar(out=out, in0=in0, scalar1=s1, scalar2=None,
                                                        op0=op0, **kw), reads=reads, writes=writes)
        return self.op(e, lambda: eng.tensor_scalar(out=out, in0=in0, scalar1=s1, scalar2=s2,
                                                    op0=op0, op1=op1, **kw), reads=reads, writes=writes)

    def stt(self, out, in0, scalar, in1, op0, op1, reads, writes, **kw):
        nc = self.nc
        return self.op("dve", lambda: nc.vector.scalar_tensor_tensor(out=out, in0=in0, scalar=scalar, in1=in1,
                                                                     op0=op0, op1=op1, **kw),
                       reads=reads, writes=writes)

    def copy(self, out, in_, reads, writes, e="dve"):
        if e == "act":
            return self.act(out, in_, AF.Copy, reads, writes)
        eng = self.eng[e]
        return self.op(e, lambda: eng.tensor_copy(out=out, in_=in_), reads=reads, writes=writes)

    def red(self, out, in_, op, reads, writes, axis=None, e="dve"):
        eng = self.eng[e]
        ax = axis if axis is not None else AX.X
        return self.op(e, lambda: eng.tensor_reduce(out=out, in_=in_, axis=ax, op=op),
                       reads=reads, writes=writes)

    def memset(self, ap, val, writes, e="dve"):
        eng = self.eng[e]
        return self.op(e, lambda: eng.memset(ap, val), writes=writes)

    def bload(self, q, tile, row_ap, n, writes):
        return self.dma(q, tile, row_ap.partition_broadcast(128), writes=writes)


def emit_mod(kb, cvT, ada_w, ada_b, col0, ncols, out_t, out_b, ident_unused=None):
    nc = kb.nc
    cs, cs_b = kb.sb([128, 8])
    crep, crep_b = kb.sb([128, 8, 128])
    kb.dma("sp", cs[:], cvT, writes=[cs_b])
    kb.act(cs[:], cs[:], AF.Silu, reads=[cs_b], writes=[cs_b])
    for k in range(8):
        kb.copy(crep[:, k, :], cs[:, k:k + 1].to_broadcast([128, 128]), reads=[cs_b], writes=[crep_b])
    wch, wch_b = kb.sb([128, 8, 512])
    bb, bb_b = kb.sb([128, 512])
    pm, pm_b = kb.ps([128, 512])
    for c0 in range(0, ncols, 512):
        kb.dma("sp", wch[:], ada_w[:, col0 + c0:col0 + c0 + 512].rearrange("(k p) n -> p k n", p=128),
               writes=[wch_b])
        kb.bload("sp", bb[:], ada_b[col0 + c0:col0 + c0 + 512], 512, writes=[bb_b])
        for k in range(8):
            kb.mm(pm[:], crep[:, k, :], wch[:, k, :], k == 0, k == 7, reads=[crep_b, wch_b], writes=[pm_b])
        kb.tt(out_t[:, c0:c0 + 512], pm[:], bb[:], ALU.add, reads=[pm_b, bb_b], writes=[out_b])


def emit_rope(kb, x, xb, H, Dh, cs, sn, csb, tmp, tmpb):
    h2 = Dh // 2
    x1 = x[:, :, 0:h2]
    x2 = x[:, :, h2:Dh]
    cb = cs.unsqueeze(1).to_broadcast([128, H, h2])
    sb_ = sn.unsqueeze(1).to_broadcast([128, H, h2])
    t1 = tmp[:, 0, 0:H, 0:h2]
    t2 = tmp[:, 1, 0:H, 0:h2]
    t3 = tmp[:, 2, 0:H, 0:h2]
    kb.tt(t1, x1, sb_, ALU.mult, reads=[xb, csb], writes=[tmpb])
    kb.tt(t2, x2, sb_, ALU.mult, reads=[xb, csb], writes=[tmpb])
    kb.tt(x1, x1, cb, ALU.mult, reads=[xb, csb], writes=[xb])
    kb.tt(x2, x2, cb, ALU.mult, reads=[xb, csb], writes=[xb])
    kb.tt(x1, x1, t2, ALU.subtract, reads=[xb, tmpb], writes=[xb])
    kb.tt(x2, x2, t1, ALU.add, reads=[xb, tmpb], writes=[xb])


def emit_rstd(kb, out, in_, scale, eps, reads, writes):
    kb.ts(out, in_, scale, eps, ALU.mult, ALU.add, reads=reads, writes=writes)
    kb.act(out, out, AF.Sqrt, reads=writes, writes=writes)
    nc = kb.nc
    kb.op("dve", lambda: nc.vector.reciprocal(out=out, in_=out), reads=writes, writes=writes)


A_EVEN_W = 768 + 1024 + 32 + 8 + 8 + 512 * 4
A_ODD_W = 4096 + 16 + 16


def build_A(even, NT, NL):
    nc = bass.Bass("TRN2", target_bir_lowering=False)
    dr = lambda n, s, k="ExternalInput": nc.dram_tensor(n, list(s), F32, kind=k).ap()
    NIN = 2464 if even else 4128
    WOUT = A_EVEN_W if even else A_ODD_W
    x = dr("x", [NT, 128, D])
    cvl = dr("cvl", [128, 8])
    cvc = dr("cvc", [128, 8])
    ada_w = dr("ada_w", [D, 2048])
    ada_b = dr("ada_b", [2048])
    w_in = dr("w_in", [D, NIN])
    ident_d = dr("ident", [128, 128])
    if even:
        qn_d = dr("q_norm", [256])
        wuq_d = dr("w_uq", [256, 768])
        kvn_d = dr("kv_norm", [128])
        wukv_d = dr("w_ukv", [128, 1024])
        ropem = dr("ropem", [NT, 128, 32])
        roper = dr("roper", [NT, 128, 64])
    else:
        alog_d = dr("a_log", [16])
        dtb_d = dr("dt_bias", [16])
    out = dr("out", [NT, 128, WOUT], "ExternalOutput")

    with ExitStack() as st:
        kb = KB(nc, st)
        ident, ident_b = kb.sb([128, 128])
        kb.dma("sp", ident[:], ident_d, writes=[ident_b])
        modl, modl_b = kb.sb([128, 2048])
        modc, modc_b = kb.sb([128, 2048])
        emit_mod(kb, cvl, ada_w, ada_b, 0, 2048, modl, modl_b)
        if NT > NL:
            emit_mod(kb, cvc, ada_w, ada_b, 0, 2048, modc, modc_b)
        kb.ts(modl[:, 1024:2048], modl[:, 1024:2048], 1.0, None, ALU.add, reads=[modl_b], writes=[modl_b])
        if NT > NL:
            kb.ts(modc[:, 1024:2048], modc[:, 1024:2048], 1.0, None, ALU.add, reads=[modc_b], writes=[modc_b])

        xt, xt_b = kb.sb([128, D])
        hT, hT_b = kb.sb([128, 8, 128])
        z, z_b = kb.sb([128, WOUT])
        wch = [kb.sb([128, 8, 512]) for _ in range(2)]
        pT = [kb.ps([128, 512]) for _ in range(2)]
        pz = [kb.ps([128, 512]) for _ in range(2)]
        if even:
            qn, qn_b = kb.sb([128, 256])
            kvn, kvn_b = kb.sb([128, 128])
            kb.bload("sp", qn[:], qn_d, 256, writes=[qn_b])
            kb.bload("sp", kvn[:], kvn_d, 128, writes=[kvn_b])
            wuq, wuq_b = kb.sb([128, 2, 768])
            wukv, wukv_b = kb.sb([128, 1024])
            kb.dma("sp", wuq[:], wuq_d.rearrange("(k p) n -> p k n", p=128), writes=[wuq_b])
            kb.dma("sp", wukv[:], wukv_d, writes=[wukv_b])
            zin, zin_b = kb.sb([128, 2464])
            rm, rm_b = kb.sb([128, 32])
            rr, rr_b = kb.sb([128, 64])
            tmp, tmp_b = kb.sb([128, 3, 8, 32])
            sm, sm_b = kb.sb([128, 8])
            cn, cn_b = kb.sb([128, 256])
            cnT, cnT_b = kb.sb([128, 3, 128])
            sq, sq_b = kb.sb([128, 1024])
            pq = [kb.ps([128, 512]) for _ in range(2)]
        else:
            alog, alog_b = kb.sb([128, 16])
            dtb, dtb_b = kb.sb([128, 16])
            kb.bload("sp", alog[:], alog_d, 16, writes=[alog_b])
            kb.bload("sp", dtb[:], dtb_d, 16, writes=[dtb_b])
            kb.act(alog[:], alog[:], AF.Exp, reads=[alog_b], writes=[alog_b])
            kb.ts(alog[:], alog[:], -1.0, None, ALU.mult, reads=[alog_b], writes=[alog_b])
            t16 = [kb.sb([128, 16]) for _ in range(3)]

        ncol_chunks = [(c, min(512, NIN - c)) for c in range(0, NIN, 512)]
        wi = 0
        for t in range(NT):
            mod, mod_b = (modl, modl_b) if t < NL else (modc, modc_b)
            kb.dma("sp", xt[:], x[t], writes=[xt_b])
            kb.tt(xt[:], xt[:], mod[:, 1024:2048], ALU.mult, reads=[xt_b, mod_b], writes=[xt_b])
            kb.tt(xt[:], xt[:], mod[:, 0:1024], ALU.add, reads=[xt_b, mod_b], writes=[xt_b])
            for half in range(2):
                p, p_b = pT[half]
                for k in range(4):
                    kk = half * 4 + k
                    kb.tr(p[:, k * 128:(k + 1) * 128], xt[:, kk * 128:(kk + 1) * 128], ident[:],
                          reads=[xt_b, ident_b], writes=[p_b])
                kb.copy(hT[:, half * 4:half * 4 + 4, :], p[:].rearrange("p (k n) -> p k n", k=4),
                        reads=[p_b], writes=[hT_b], e="act")
            zdst, zdst_b = (zin, zin_b) if even else (z, z_b)
            for (c0, cw) in ncol_chunks:
                w, w_b = wch[wi % 2]
                pp, pp_b = pz[wi % 2]
                wi += 1
                kb.dma("sp", w[:, :, 0:cw], w_in[:, c0:c0 + cw].rearrange("(k p) n -> p k n", p=128),
                       writes=[w_b])
                for k in range(8):
                    kb.mm(pp[:, 0:cw], hT[:, k, :], w[:, k, 0:cw], k == 0, k == 7,
                          reads=[hT_b, w_b], writes=[pp_b])
                kb.copy(zdst[:, c0:c0 + cw], pp[:, 0:cw], reads=[pp_b], writes=[zdst_b],
                        e=("act" if (wi % 2) else "dve"))
            if even:
                kb.dma("sp", rm[:], ropem[t], writes=[rm_b])
                kb.dma("sp", rr[:], roper[t], writes=[rr_b])
                kb.act(sq[:, 0:256], zin[:, 0:256], AF.Square, reads=[zin_b], writes=[sq_b, sm_b],
                       accum_out=sm[:, 0:1])
                kb.act(sq[:, 0:128], zin[:, 256:384], AF.Square, reads=[zin_b], writes=[sq_b, sm_b],
                       accum_out=sm[:, 1:2])
                emit_rstd(kb, sm[:, 0:1], sm[:, 0:1], 1.0 / 256, 1e-6, reads=[sm_b], writes=[sm_b])
                emit_rstd(kb, sm[:, 1:2], sm[:, 1:2], 1.0 / 128, 1e-6, reads=[sm_b], writes=[sm_b])
                kb.stt(cn[:, 0:256], zin[:, 0:256], sm[:, 0:1], qn[:], ALU.mult, ALU.mult,
                       reads=[zin_b, sm_b, qn_b], writes=[cn_b])
                p, p_b = pT[0]
                for k in range(2):
                    kb.tr(p[:, k * 128:(k + 1) * 128], cn[:, k * 128:(k + 1) * 128], ident[:],
                          reads=[cn_b, ident_b], writes=[p_b])
                kb.copy(cnT[:, 0:2, :], p[:, 0:256].rearrange("p (k n) -> p k n", k=2),
                        reads=[p_b], writes=[cnT_b], e="act")
                kb.stt(cn[:, 0:128], zin[:, 256:384], sm[:, 1:2], kvn[:], ALU.mult, ALU.mult,
                       reads=[zin_b, sm_b, kvn_b], writes=[cn_b])
                p, p_b = pT[1]
                kb.tr(p[:, 0:128], cn[:, 0:128], ident[:], reads=[cn_b, ident_b], writes=[p_b])
                kb.copy(cnT[:, 2, :], p[:, 0:128], reads=[p_b], writes=[cnT_b], e="act")
                for (c0, cw) in ((0, 512), (512, 256)):
                    pp, pp_b = pq[0] if c0 == 0 else pq[1]
                    for k in range(2):
                        kb.mm(pp[:, 0:cw], cnT[:, k, :], wuq[:, k, c0:c0 + cw], k == 0, k == 1,
                              reads=[cnT_b, wuq_b], writes=[pp_b])
                    kb.copy(z[:, c0:c0 + cw], pp[:, 0:cw], reads=[pp_b], writes=[z_b])
                for j in range(2):
                    pp, pp_b = pq[j]
                    kb.mm(pp[:], cnT[:, 2, :], wukv[:, j * 512:(j + 1) * 512], True, True,
                          reads=[cnT_b, wukv_b], writes=[pp_b])
                    kb.copy(z[:, 768 + j * 512:768 + (j + 1) * 512], pp[:], reads=[pp_b], writes=[z_b], e="act")
                kb.copy(z[:, 1792:1824], zin[:, 384:416], reads=[zin_b], writes=[z_b])
                kb.copy(z[:, 1840:3888], zin[:, 416:2464], reads=[zin_b], writes=[z_b], e="pool")
                qv = z[:, 0:768].rearrange("p (h d) -> p h d", h=8)
                emit_rope(kb, qv[:, :, 64:96], z_b, 8, 32, rm[:, 0:16], rm[:, 16:32], rm_b, tmp, tmp_b)
                krv = z[:, 1792:1824].rearrange("p (h d) -> p h d", h=1)
                emit_rope(kb, krv, z_b, 1, 32, rm[:, 0:16], rm[:, 16:32], rm_b, tmp, tmp_b)
                rqv = z[:, 1840:2352].rearrange("p (h d) -> p h d", h=8)
                emit_rope(kb, rqv, z_b, 8, 64, rr[:, 0:32], rr[:, 32:64], rr_b, tmp, tmp_b)
                rkv = z[:, 2352:2864].rearrange("p (h d) -> p h d", h=8)
                emit_rope(kb, rkv, z_b, 8, 64, rr[:, 0:32], rr[:, 32:64], rr_b, tmp, tmp_b)
                kb.ts(z[:, 2352:2864], z[:, 2352:2864], 0.125, None, ALU.mult, reads=[z_b], writes=[z_b])
                kb.tt(sq[:, 0:768], z[:, 0:768], z[:, 0:768], ALU.mult, reads=[z_b], writes=[sq_b])
                kb.red(z[:, 1824:1832], sq[:, 0:768].rearrange("p (h d) -> p h d", h=8), ALU.add,
                       reads=[sq_b], writes=[z_b])
                kvv = z[:, 768:1792].rearrange("p (h d) -> p h d", h=8)
                sqv = sq[:, 0:512].rearrange("p (h d) -> p h d", h=8)
                kb.tt(sqv, kvv[:, :, 0:64], kvv[:, :, 0:64], ALU.mult, reads=[z_b], writes=[sq_b])
                kb.red(z[:, 1832:1840], sqv, ALU.add, reads=[sq_b], writes=[z_b])
                kb.tt(sq[:, 0:32], z[:, 1792:1824], z[:, 1792:1824], ALU.mult, reads=[z_b], writes=[sq_b])
                kb.red(sm[:, 2:3], sq[:, 0:32], ALU.add, reads=[sq_b], writes=[sm_b])
                kb.ts(z[:, 1832:1840], z[:, 1832:1840], sm[:, 2:3], None, ALU.add, reads=[z_b, sm_b], writes=[z_b])
            else:
                (xa, xa_b), (ta, ta_b), (tb, tb_b) = t16
                kb.tt(xa[:], z[:, 4096:4112], dtb[:], ALU.add, reads=[z_b, dtb_b], writes=[xa_b])
                kb.ts(ta[:], xa[:], -1.0, None, ALU.mult, reads=[xa_b], writes=[ta_b])
                kb.tt(ta[:], ta[:], xa[:], ALU.max, reads=[xa_b, ta_b], writes=[ta_b])
                kb.act(ta[:], ta[:], AF.Exp, reads=[ta_b], writes=[ta_b], scale=-1.0)
                kb.ts(ta[:], ta[:], 1.0, None, ALU.add, reads=[ta_b], writes=[ta_b])
                kb.act(ta[:], ta[:], AF.Ln, reads=[ta_b], writes=[ta_b])
                kb.ts(xa[:], xa[:], 0.0, None, ALU.max, reads=[xa_b], writes=[xa_b])
                kb.tt(xa[:], xa[:], ta[:], ALU.add, reads=[xa_b, ta_b], writes=[xa_b])
                kb.act(tb[:], z[:, 4112:4128], AF.Sigmoid, reads=[z_b], writes=[tb_b])
                kb.tt(z[:, 4096:4112], xa[:], alog[:], ALU.mult, reads=[xa_b, alog_b], writes=[z_b])
                kb.copy(z[:, 4112:4128], tb[:], reads=[tb_b], writes=[z_b])
            if t == 0:
                ob = Buf("out")
            kb.dma("sp", out[t], z[:], reads=[z_b], writes=[ob], nowaw=True)
        kb.finish([ob])
    return nc


MLA_SCALE = 96.0 ** -0.5


def build_ME(NKT, NCT, HH=4):
    nc = bass.Bass("TRN2", target_bir_lowering=False)
    dr = lambda n, s, k="ExternalInput": nc.dram_tensor(n, list(s), F32, kind=k).ap()
    NK = NKT * 128
    NC = NCT * 128
    QT = dr("QT", [HH, 128, NK])
    KT = dr("KT", [HH, 128, NK])
    VA = dr("VA", [HH, 128, NKT, 65])
    KN2 = dr("KN2", [HH, 1, NK])
    SEL = dr("SEL", [65, 64])
    RQT = dr("RQT", [HH, 64, NK])
    RKT = dr("RKT", [HH, 64, NK])
    RK = dr("RK", [HH, 128, NKT, 64])
    RV = dr("RV", [HH, 128, NKT, 64])
    DSYM = dr("DSYM", [HH, 128, 128])
    DCOL = dr("DCOL", [128, HH, 4])
    GC = dr("GC", [64, HH])
    OT = dr("OT", [HH, 64, NK], "ExternalOutput")
    RO = dr("RO", [HH, 128, NKT, 64], "ExternalOutput")
    obuf = Buf("outs")

    with ExitStack() as st:
        kb = KB(nc, st)
        G1, G1b = kb.sb([128, NK])
        G2, G2b = kb.sb([128, NK])
        G3, G3b = kb.sb([128, NKT * 65])
        G4, G4b = kb.sb([128, NKT * 65])
        G5, G5b = kb.sb([128, NKT * 64])
        SF, SFb = kb.sb([64, NKT + 1, 64])
        SB_, SBb = kb.sb([64, NKT + 1, 64])
        sel, selb = kb.sb([65, 64])
        kb.dma("sp", sel[:], SEL, writes=[selb])
        dcol, dcolb = kb.sb([128, HH, 4])
        kb.dma("sp", dcol[:], DCOL, writes=[dcolb])
        gc, gcb = kb.sb([64, HH])
        kb.dma("sp", gc[:], GC, writes=[gcb])
        kn2, kn2b = kb.sb([1, NK])
        km, kmb = kb.sb([1, 1])
        qblk = [kb.sb([128, 512]) for _ in range(2)]
        pts = [kb.sb([128, 512]) for _ in range(3)]
        oa, oab = kb.sb([65, 512])
        rb, rbb = kb.sb([64, 512])
        ob_, obb = kb.sb([64, 512])
        pS = [kb.ps([128, 512]) for _ in range(2)]
        pO, pOb = kb.ps([65, 512])
        pB, pBb = kb.ps([64, 512])
        qblocks = [(0, NC, 0, NCT)] if NCT > 0 else []
        for c0 in range(NC, NK, 512):
            qblocks.append((c0, min(512, NK - c0), 0, NKT))
        it = 0
        for h in range(HH):
            kb.dma("sp", G1[:], KT[h], writes=[G1b])
            kb.dma("sp", G4[:].rearrange("p (n d) -> p n d", d=65), VA[h], writes=[G4b])
            kb.dma("sp", kn2[:], KN2[h], writes=[kn2b])
            kb.red(km[:], kn2[:], ALU.max, reads=[kn2b], writes=[kmb])
            va = G4[:].rearrange("p (n d) -> p n d", d=65)
            for (c0, cw, k0, k1) in qblocks:
                q, qb_ = qblk[it % 2]
                it += 1
                kb.dma("sp", q[:, 0:cw], QT[h][:, c0:c0 + cw], writes=[qb_])
                kb.ts(q[0:1, 0:cw], q[0:1, 0:cw], km[0:1, 0:1], None, ALU.mult, reads=[qb_, kmb], writes=[qb_])
                kb.act(q[0:1, 0:cw], q[0:1, 0:cw], AF.Sqrt, reads=[qb_], writes=[qb_])
                kb.ts(q[0:1, 0:cw], q[0:1, 0:cw], -1.0, None, ALU.mult, reads=[qb_], writes=[qb_])
                nkt = k1 - k0

                def smm(i):
                    ps, psb = pS[i % 2]
                    kt = k0 + i
                    kb.mm(ps[:, 0:cw], G1[:, kt * 128:(kt + 1) * 128], q[:, 0:cw], True, True,
                          reads=[G1b, qb_], writes=[psb])
                smm(0)
                for i in range(nkt):
                    ps, psb = pS[i % 2]
                    pt, ptb = pts[i % 3]
                    kb.act(pt[:, 0:cw], ps[:, 0:cw], AF.Exp, reads=[psb], writes=[ptb], scale=MLA_SCALE)
                    if i + 1 < nkt:
                        smm(i + 1)
                    kb.mm(pO[:, 0:cw], va[:, k0 + i, :], pt[:, 0:cw], i == 0, i == nkt - 1,
                          reads=[G4b, ptb], writes=[pOb])
                kb.copy(oa[:, 0:cw], pO[:, 0:cw], reads=[pOb], writes=[oab])
                kb.mm(pB[:, 0:cw], sel[:], oa[:, 0:cw], True, True, reads=[selb, oab], writes=[pBb])
                kb.op("dve", lambda: nc.vector.reciprocal(out=rb[:, 0:cw], in_=pB[:, 0:cw]),
                      reads=[pBb], writes=[rbb])
                kb.tt(ob_[:, 0:cw], oa[0:64, 0:cw], rb[:, 0:cw], ALU.mult, reads=[oab, rbb], writes=[obb])
                kb.dma("sp", OT[h][:, c0:c0 + cw], ob_[:, 0:cw], reads=[obb], writes=[obuf], nowaw=True)
        dsym, dsymb = kb.sb([128, 128])
        at = [kb.sb([128, 128]) for _ in range(2)]
        osb = [kb.sb([128, 64]) for _ in range(2)]
        pKV, pKVb = kb.ps([64, 512])
        pR = pS
        rk = G3[:, 0:NKT * 64].rearrange("p (n d) -> p n d", d=64)
        rv = G4[:, 0:NKT * 64].rearrange("p (n d) -> p n d", d=64)
        kd = G5[:].rearrange("p (n d) -> p n d", d=64)
        fwd_order = list(range(NKT))
        bwd_order = list(range(NCT - 1, -1, -1)) + list(range(NKT - 1, NCT - 1, -1))
        for h in range(HH):
            kb.dma("sp", G1[0:64, :], RQT[h], writes=[G1b])
            kb.dma("sp", G2[0:64, :], RKT[h], writes=[G2b])
            kb.dma("sp", rk, RK[h], writes=[G3b])
            kb.dma("sp", rv, RV[h], writes=[G4b])
            kb.dma("sp", dsym[:], DSYM[h], writes=[dsymb])
            for (di, order, S_, S_b) in ((0, fwd_order, SF, SFb), (1, bwd_order, SB_, SBb)):
                kb.ts(G5[:], G3[:, 0:NKT * 64], dcol[:, h, di:di + 1], None, ALU.mult,
                      reads=[G3b, dcolb], writes=[G5b])
                kb.memset(S_[:, order[0], :], 0.0, writes=[S_b])
                for g0 in range(0, NKT, 8):
                    grp = order[g0:g0 + 8]
                    for j, c in enumerate(grp):
                        kb.mm(pKV[:, j * 64:(j + 1) * 64], kd[:, c, :], rv[:, c, :], True, True,
                              reads=[G5b, G4b], writes=[pKVb])
                    for j, c in enumerate(grp):
                        idx = g0 + j
                        nxt = order[idx + 1] if idx + 1 < NKT else NKT
                        kb.stt(S_[:, nxt, :], S_[:, c, :], gc[:, h:h + 1], pKV[:, j * 64:(j + 1) * 64],
                               ALU.mult, ALU.add, reads=[S_b, gcb, pKVb], writes=[S_b])
            for n in range(NKT):
                pr, prb = pR[n % 2]
                a_, a_b = at[n % 2]
                o_, o_b = osb[n % 2]
                sl = slice(n * 128, (n + 1) * 128)
                kb.mm(pr[:, 0:128], G2[0:64, sl], G1[0:64, sl], True, True, reads=[G1b, G2b], writes=[prb])
                kb.tt(a_[:], pr[:, 0:128], dsym[:], ALU.mult, reads=[prb, dsymb], writes=[a_b])
                kb.mm(pr[:, 128:192], a_[:], rv[:, n, :], True, True, reads=[a_b, G4b], writes=[prb])
                kb.mm(pr[:, 192:256], G1[0:64, sl], SF[:, n, :], True, True, reads=[G1b, SFb], writes=[prb])
                kb.mm(pr[:, 256:320], G1[0:64, sl], SB_[:, n, :], True, True, reads=[G1b, SBb], writes=[prb])
                kb.copy(o_[:], pr[:, 128:192], reads=[prb], writes=[o_b], e="act")
                kb.stt(o_[:], pr[:, 192:256], dcol[:, h, 2:3], o_[:], ALU.mult, ALU.add,
                       reads=[prb, dcolb, o_b], writes=[o_b])
                kb.stt(o_[:], pr[:, 256:320], dcol[:, h, 3:4], o_[:], ALU.mult, ALU.add,
                       reads=[prb, dcolb, o_b], writes=[o_b])
                kb.dma("sp", RO[h][:, n, :], o_[:], reads=[o_b], writes=[obuf], nowaw=True)
        kb.finish([obuf])
    return nc


class Pool:
    def __init__(self, items):
        self.items = items
        self.i = 0

    def get(self):
        it = self.items[self.i % len(self.items)]
        self.i += 1
        return it


def roundrobin(gens):
    gens = list(gens)
    while gens:
        nxt = []
        for g in gens:
            try:
                next(g)
                nxt.append(g)
            except StopIteration:
                pass
        gens = nxt


def build_MO(NKT, NCT, HH=4):
    nc = bass.Bass("TRN2", target_bir_lowering=False)
    dr = lambda n, s, k="ExternalInput": nc.dram_tensor(n, list(s), F32, kind=k).ap()
    NK = NKT * 128
    NC = NCT * 128
    PZ = dr("PZ", [3, HH, 128, NK + 8])
    CW = dr("CW", [HH, 128, 3, 5])
    GG = dr("GG", [128, 2, NKT, HH])
    BB = dr("BB", [128, 2, NKT, HH])
    CM = dr("CM", [128, 10, 128])
    OUT = dr("OUT", [2, HH, 128, NKT, 128], "ExternalOutput")
    obuf = Buf("outs")
    NS = NKT * HH

    with ExitStack() as st:
        kb = KB(nc, st)
        cm, cmb = kb.sb([128, 10, 128])
        kb.dma("sp", cm[:], CM, writes=[cmb])
        TRI = lambda d: cm[:, 4 * d + 0, :]
        UU = lambda d: cm[:, 4 * d + 1, :]
        MINC = lambda d: cm[:, 4 * d + 2, :]
        MSTR = lambda d: cm[:, 4 * d + 3, :]
        ident = cm[:, 8, :]
        ones = cm[:, 9, :]
        QT, QTb = kb.sb([128, NK])
        KT, KTb = kb.sb([128, NK])
        VTOK, VTOKb = kb.sb([128, NKT, 128])
        import os
        NSP = NS
        gg, ggb = kb.sb([128, 2 * NSP])
        bt, btb = kb.sb([128, 2 * NSP])
        ee, eeb = kb.sb([128, 2 * NSP])
        ne, neb = kb.sb([128, 2 * NSP])
        dd, ddb = kb.sb([128, 2 * NSP])
        cd, cdb = kb.sb([128, 2 * NSP])
        if os.environ.get("V1") != "1":
            kb.memset(gg[:], 0.0, writes=[ggb])
            kb.memset(bt[:], 0.0, writes=[btb])
        for d in range(2):
            kb.dma("sp", gg[:, d * NSP:d * NSP + NS], GG[:, d].rearrange("p n h -> p (n h)"), writes=[ggb])
            kb.dma("sp", bt[:, d * NSP:d * NSP + NS], BB[:, d].rearrange("p n h -> p (n h)"), writes=[btb])
        big = [kb.ps([128, 512]) for _ in range(2)]
        for d in range(2):
            p1, p1b = big[0]
            p2, p2b = big[1]
            sl = slice(d * NSP, (d + 1) * NSP)
            for c0 in range(0, NSP, 512):
                cwd = min(512, NSP - c0)
                s2 = slice(d * NSP + c0, d * NSP + c0 + cwd)
                PRO = int(os.environ.get("PRO", "9"))
                if PRO >= 2:
                    kb.mm(p1[:, 0:cwd], TRI(d), gg[:, s2], True, True, reads=[cmb, ggb], writes=[p1b])
                    kb.mm(p2[:, 0:cwd], ones, gg[:, s2], True, True, reads=[cmb, ggb], writes=[p2b])
                if PRO >= 3:
                    kb.act(ee[:, s2], p1[:, 0:cwd], AF.Exp, reads=[p1b], writes=[eeb])
                    kb.act(cd[:, s2], p2[:, 0:cwd], AF.Exp, reads=[p2b], writes=[cdb])
                if PRO >= 4:
                    kb.copy(dd[:, s2], p2[:, 0:cwd], reads=[p2b], writes=[ddb], e="act")
                    kb.copy(ne[:, s2], p1[:, 0:cwd], reads=[p1b], writes=[neb], e="act")
                    kb.tt(dd[:, s2], dd[:, s2], ne[:, s2], ALU.subtract, reads=[ddb, neb], writes=[ddb])
                if PRO >= 5:
                    kb.act(dd[:, s2], dd[:, s2], AF.Exp, reads=[ddb], writes=[ddb])
        if PRO >= 6:
            kb.ts(ne[:], ee[:], -1.0, None, ALU.mult, reads=[eeb], writes=[neb])
        col = lambda t, d, n, h: t[:, d * NSP + n * HH + h: d * NSP + n * HH + h + 1]

        cw, cwb = kb.sb([128, 3, 5])
        pin = [kb.sb([128, 516]) for _ in range(2)]
        yb = [kb.sb([128, 512]) for _ in range(2)]
        y2, y2b = kb.sb([128, 512])
        rs, rsb = kb.sb([128, 512])
        qtiles = [kb.ps([128, 512]) for _ in range(5)]
        pslots = Pool([(t[:, j * 128:(j + 1) * 128], b) for j in range(4) for (t, b) in qtiles])
        mt = [kb.sb([128, 128]) for _ in range(44)]
        mpool = Pool([(t[:], b) for (t, b) in mt])
        S = [kb.sb([128, 128]) for _ in range(2)]
        fin = []
        for d in range(2):
            fin.append([])
            for j in range(2):
                a1, b1 = kb.sb([128, 128]); a2, b2 = kb.sb([128, 128]); a3, b3 = kb.sb([128, 128])
                fin[d].append((a1[:], b1, a2[:], b2, a3[:], b3))
        orders = [list(range(NKT)),
                  list(range(NCT - 1, -1, -1)) + list(range(NKT - 1, NCT - 1, -1))]

        blocks = ([(0, NC, 0)] if NCT else []) + [(c0, min(512, NK - c0), 4) for c0 in range(NC, NK, 512)]
        import os
        STG = int(os.environ.get("MO_STAGE", "9"))
        SUB = int(os.environ.get("MO_SUB", "99"))
        for h in range(HH if STG >= 1 else 0):
            kb.dma("sp", cw[:], CW[h], writes=[cwb])
            bi = 0
            for (c0, cwid, off) in blocks:
                for w in range(3):
                    pi, pib = pin[bi % 2]
                    y, ybb = yb[bi % 2]
                    bi += 1
                    kb.dma("sp", pi[:, 0:cwid + 4], PZ[w, h][:, c0 + off:c0 + off + cwid + 4], writes=[pib])
                    kb.ts(y[:, 0:cwid], pi[:, 0:cwid], cw[:, w, 0:1], None, ALU.mult, reads=[pib, cwb], writes=[ybb])
                    for j in range(1, 5):
                        kb.stt(y[:, 0:cwid], pi[:, j:j + cwid], cw[:, w, j:j + 1], y[:, 0:cwid], ALU.mult, ALU.add,
                               reads=[pib, cwb, ybb], writes=[ybb])
                    kb.act(y[:, 0:cwid], y[:, 0:cwid], AF.Silu, reads=[ybb], writes=[ybb])
                    if w < 2:
                        pb_, pbb = big[0]
                        kb.act(y2[:, 0:cwid], y[:, 0:cwid], AF.Square, reads=[ybb], writes=[y2b])
                        kb.mm(pb_[:, 0:cwid], ones, y2[:, 0:cwid], True, True, reads=[cmb, y2b], writes=[pbb])
                        kb.ts(rs[:, 0:cwid], pb_[:, 0:cwid], 1e-6, None, ALU.add, reads=[pbb], writes=[rsb])
                        kb.act(rs[:, 0:cwid], rs[:, 0:cwid], AF.Sqrt, reads=[rsb], writes=[rsb])
                        kb.op("dve", lambda: nc.vector.reciprocal(out=rs[:, 0:cwid], in_=rs[:, 0:cwid]),
                              reads=[rsb], writes=[rsb])
                        dst, dstb = (QT, QTb) if w == 0 else (KT, KTb)
                        if w == 0:
                            kb.stt(dst[:, c0:c0 + cwid], y[:, 0:cwid], 128.0 ** -0.5, rs[:, 0:cwid], ALU.mult, ALU.mult,
                                   reads=[ybb, rsb], writes=[dstb])
                        else:
                            kb.tt(dst[:, c0:c0 + cwid], y[:, 0:cwid], rs[:, 0:cwid], ALU.mult,
                                  reads=[ybb, rsb], writes=[dstb])
                    if w == 2:
                        src, srcb = (y[:, 0:cwid], ybb)
                        dst, dstb = (VTOK, VTOKb)
                        pb_, pbb = big[1]
                        for j in range(cwid // 128):
                            kb.tr(pb_[:, j * 128:(j + 1) * 128], src[:, j * 128:(j + 1) * 128], ident,
                                  reads=[srcb, cmb], writes=[pbb])
                        kb.copy(dst[:, c0 // 128:(c0 + cwid) // 128, :],
                                pb_[:, 0:cwid].rearrange("p (n d) -> p n d", d=128), reads=[pbb], writes=[dstb], e="act")

            if STG < 2:
                continue
            pre_done = [0, 0]
            scan_pos = [0, 0]

            def precompute(i, d):
                n = orders[d][i]
                while scan_pos[d] < i - 1 and STG >= 3:
                    yield
                fMT, fMTb, fAT, fATb, fKD, fKDb = fin[d][i % 2]
                sl = slice(n * 128, (n + 1) * 128)
                pkk, pkkb = pslots.get()
                pqk, pqkb = pslots.get()
                kb.mm(pkk, KT[:, sl], KT[:, sl], True, True, reads=[KTb], writes=[pkkb])
                kb.mm(pqk, KT[:, sl], QT[:, sl], True, True, reads=[KTb, QTb], writes=[pqkb])
                B, Bb = mpool.get()
                kb.ts(B, UU(d), col(gg, d, n, h), None, ALU.mult, reads=[cmb, ggb], writes=[Bb])
                pg, pgb = pslots.get()
                kb.mm(pg, B, TRI(d), True, True, reads=[Bb, cmb], writes=[pgb])
                yield
                if SUB <= 1:
                    pre_done[d] = i + 1
                    return
                E, Eb = mpool.get()
                kb.act(E, pg, AF.Exp, reads=[pgb], writes=[Eb])
                yield
                if SUB <= 2:
                    pre_done[d] = i + 1
                    return
                Es, Esb = mpool.get()
                Ei, Eib = mpool.get()
                kb.tt(Es, E, MSTR(d), ALU.mult, reads=[Eb, cmb], writes=[Esb])
                kb.tt(Ei, E, MINC(d), ALU.mult, reads=[Eb, cmb], writes=[Eib])
                yield
                if SUB <= 3:
                    pre_done[d] = i + 1
                    return
                X, Xb = mpool.get()
                kb.stt(X, pkk, col(bt, d, n, h), Es, ALU.mult, ALU.mult, reads=[pkkb, btb, Esb], writes=[Xb])
                kb.tt(fAT, pqk, Ei, ALU.mult, reads=[pqkb, Eib], writes=[fATb])
                yield
                if SUB <= 4:
                    pre_done[d] = i + 1
                    return
                pk, pkb = pslots.get()
                kb.tr(pk, KT[:, sl], ident, reads=[KTb, cmb], writes=[pkb])
                kb.ts(fKD, pk, col(dd, d, n, h), None, ALU.mult, reads=[pkb, ddb], writes=[fKDb])
                px, pxb = pslots.get()
                kb.tr(px, X, ident, reads=[Xb, cmb], writes=[pxb])
                XT, XTb = mpool.get()
                kb.copy(XT, px, reads=[pxb], writes=[XTb], e="act")
                P, Pb = mpool.get()
                kb.tt(P, ident, X, ALU.subtract, reads=[cmb, Xb], writes=[Pb])
                yield
                if SUB <= 5:
                    pre_done[d] = i + 1
                    return
                A, Ab, ATr, ATrb = X, Xb, XT, XTb
                for j in range(6):
                    last = j == 5
                    p2t, p2tb = pslots.get()
                    kb.mm(p2t, A, ATr, True, True, reads=[Ab, ATrb], writes=[p2tb])
                    if not last:
                        p2, p2b_ = pslots.get()
                        kb.mm(p2, ATr, A, True, True, reads=[Ab, ATrb], writes=[p2b_])
                    yield
                    if SUB <= 6:
                        pre_done[d] = i + 1
                        return
                    nT, nTb = mpool.get()
                    kb.copy(nT, p2t, reads=[p2tb], writes=[nTb], e="act")
                    if not last:
                        nA, nAb = mpool.get()
                        kb.copy(nA, p2, reads=[p2b_], writes=[nAb])
                    yield
                    if SUB <= 7:
                        pre_done[d] = i + 1
                        return
                    pp, ppb = pslots.get()
                    kb.mm(pp, nT, P, True, True, reads=[nTb, Pb], writes=[ppb])
                    yield
                    if SUB <= 8:
                        pre_done[d] = i + 1
                        return
                    nP, nPb = (fMT, fMTb) if last else mpool.get()
                    kb.tt(nP, pp, P, ALU.add, reads=[ppb, Pb], writes=[nPb])
                    P, Pb = nP, nPb
                    if not last:
                        A, Ab, ATr, ATrb = nA, nAb, nT, nTb
                    yield
                    if SUB <= 9:
                        pre_done[d] = i + 1
                        return
                pre_done[d] = i + 1

            def scan(d):
                S_, Sb_ = S[d]
                kb.memset(S_[:], 0.0, writes=[Sb_], e="pool")
                for i, n in enumerate(orders[d]):
                    while pre_done[d] < i + 1:
                        yield
                    MT, MTb, AT, ATb, KD, KDb = fin[d][i % 2]
                    sl = slice(n * 128, (n + 1) * 128)
                    pks, pksb = pslots.get()
                    pqs, pqsb = pslots.get()
                    kb.mm(pks, KT[:, sl], S_[:], True, True, reads=[KTb, Sb_], writes=[pksb])
                    kb.mm(pqs, QT[:, sl], S_[:], True, True, reads=[QTb, Sb_], writes=[pqsb])
                    yield
                    R, Rb = mpool.get()
                    kb.stt(R, pks, col(ne, d, n, h), VTOK[:, n, :], ALU.mult, ALU.add,
                           reads=[pksb, neb, VTOKb], writes=[Rb])
                    yield
                    pmr, pmrb = pslots.get()
                    kb.mm(pmr, MT, R, True, True, reads=[MTb, Rb], writes=[pmrb])
                    yield
                    VN, VNb = mpool.get()
                    kb.ts(VN, pmr, col(bt, d, n, h), None, ALU.mult, reads=[pmrb, btb], writes=[VNb])
                    yield
                    pav, pavb = pslots.get()
                    psn, psnb = pslots.get()
                    kb.mm(psn, KD, VN, True, True, reads=[KDb, VNb], writes=[psnb])
                    kb.mm(pav, AT, VN, True, True, reads=[ATb, VNb], writes=[pavb])
                    yield
                    kb.stt(S_[:], S_[:], col(cd, d, n, h), psn, ALU.mult, ALU.add,
                           reads=[Sb_, cdb, psnb], writes=[Sb_])
                    O1, O1b = mpool.get()
                    kb.copy(O1, pav, reads=[pavb], writes=[O1b], e="act")
                    yield
                    kb.stt(O1, pqs, col(ee, d, n, h), O1, ALU.mult, ALU.add, reads=[pqsb, eeb, O1b], writes=[O1b])
                    kb.dma("sp", OUT[d, h][:, n, :], O1, reads=[O1b], writes=[obuf], nowaw=True)
                    scan_pos[d] = i + 1
                    yield

            def pre_all(d):
                for i in range(NKT):
                    yield from precompute(i, d)

            if STG < 3:
                roundrobin([pre_all(0), pre_all(1)])
            else:
                roundrobin([pre_all(0), pre_all(1), scan(0), scan(1)])
        kb.finish([obuf])
    return nc


def mo_consts():
    j = np.arange(128)[:, None]
    c = np.arange(128)[None, :]
    m = np.zeros((128, 10, 128), np.float32)
    m[:, 0] = (j <= c)
    m[:, 1] = (j > c)
    m[:, 2] = (j <= c)
    m[:, 3] = (j < c)
    m[:, 4] = (j >= c)
    m[:, 5] = (j < c)
    m[:, 6] = (j >= c)
    m[:, 7] = (j > c)
    m[:, 8] = np.eye(128)
    m[:, 9] = 1.0
    return m


def emit_ln(kb, dst, dstb, src, srcb, g, gb, b, bb, st6, st6b, mv, mvb):
    nc = kb.nc
    for j in range(2):
        kb.op("dve", lambda: nc.vector.bn_stats(out=st6[:, j, :], in_=src[:, j * 512:(j + 1) * 512]),
              reads=[srcb], writes=[st6b])
    kb.op("dve", lambda: nc.vector.bn_aggr(out=mv[:, 0:2], in_=st6[:].rearrange("p a b -> p (a b)")),
          reads=[st6b], writes=[mvb])
    emit_rstd(kb, mv[:, 1:2], mv[:, 1:2], 1.0, 1e-5, reads=[mvb], writes=[mvb])
    kb.ts(dst[:], src[:], mv[:, 0:1], mv[:, 1:2], ALU.subtract, ALU.mult, reads=[srcb, mvb], writes=[dstb])
    kb.tt(dst[:], dst[:], g[:], ALU.mult, reads=[dstb, gb], writes=[dstb])
    kb.tt(dst[:], dst[:], b[:], ALU.add, reads=[dstb, bb], writes=[dstb])


def build_C(even, NT, NL):
    nc = bass.Bass("TRN2", target_bir_lowering=False)
    dr = lambda n, s, k="ExternalInput", dt=F32: nc.dram_tensor(n, list(s), dt, kind=k).ap()
    x = dr("x", [NT, 128, D])
    if even:
        mixd = dr("mix", [NT, 128, 1024])
        gated = dr("gate", [NT, 128, 512])
        gng = dr("gn_g", [512])
    else:
        mixd = dr("mix", [2, NT, 128, 1024])
        gated = dr("gate", [NT, 128, 1024])
        gng = dr("gn_g", [128])
    w_out = dr("w_out", [D, D])
    cvl = dr("cvl", [128, 8])
    cvc = dr("cvc", [128, 8])
    ada_w = dr("ada_w", [D, 4096])
    ada_b = dr("ada_b", [4096])
    lnp = dr("lnp", [4, D])
    w_q = dr("w_q", [D, 2048])
    kTd = dr("kT", [128, 16, 128])
    u_tab = dr("u_tab", [16384, D])
    v_tab = dr("v_tab", [16384, D])
    cst = dr("cst", [128, 144])
    out = dr("out", [NT, 128, D], "ExternalOutput")
    obuf = Buf("outs")

    with ExitStack() as st:
        kb = KB(nc, st)
        cs, csb = kb.sb([128, 144])
        kb.dma("sp", cs[:], cst, writes=[csb])
        ident = cs[:, 0:128]
        iota = cs[:, 128:144]
        mod, modb = kb.sb([128, 4096])
        ln, lnb = kb.sb([128, 4, D])
        for j in range(4):
            kb.bload("sp", ln[:, j, :], lnp[j], D, writes=[lnb])
        gg, ggb = kb.sb([128, 512 if even else 128])
        kb.bload("sp", gg[:], gng, 512 if even else 128, writes=[ggb])
        kT, kTb = kb.sb([128, 16, 128])
        kb.dma("sp", kT[:], kTd, writes=[kTb])
        wch = [kb.sb([128, 8, 512]) for _ in range(2)]
        xt, xtb = kb.sb([128, D])
        mix, mixb = kb.sb([128, D])
        mix2, mix2b = kb.sb([128, D])
        gt, gtb = kb.sb([128, D])
        cT, cTb = kb.sb([128, 8, 128])
        tt_, ttb = kb.sb([128, D])
        x1, x1b = kb.sb([128, D])
        xb_, xbb = kb.sb([128, D])
        qT, qTb = kb.sb([128, 16, 128])
        ssb, ssbb = kb.sb([128, 16, 128])
        acc, accb = kb.sb([128, D])
        ub = [kb.sb([128, D]) for _ in range(2)]
        vb = [kb.sb([128, D]) for _ in range(2)]
        junk, junkb = kb.sb([128, D])
        st6, st6b = kb.sb([128, 2, 6])
        mv, mvb = kb.sb([128, 2])
        sm, smb = kb.sb([128, 16])
        vals, valsb = kb.sb([128, 16, 16])
        idxu, idxub = kb.sb([128, 16, 16], U32)
        idxf, idxfb = kb.sb([128, 16, 16])
        tmp, tmpb = kb.sb([128, 256])
        cand, candb = kb.sb([128, 256])
        sc16, sc16b = kb.sb([128, 16])
        jpu, jpub = kb.sb([128, 16], U32)
        jau, jaub = kb.sb([128, 2, 16], U32)
        jaf, jafb = kb.sb([128, 2, 16])
        oh, ohb = kb.sb([128, 16, 16])
        isel, iselb = kb.sb([128, 2, 16])
        ef, efb = kb.sb([128, 16])
        ei = [kb.sb([128, 16], I32) for _ in range(2)]
        ex, exb = kb.sb([128, 16])
        dots, dotsb = kb.sb([128, 16])
        wg = [kb.sb([128, 16]) for _ in range(2)]
        pbig = [kb.ps([128, 512]) for _ in range(4)]
        pz = [kb.ps([128, 512]) for _ in range(2)]
        NEG = -1.0e30
        wi = 0

        def top16(src, srcb, n, vout, voutb, iout, ioutb):
            kb.op("dve", lambda: nc.vector.max(out=vout[:, 0:8], in_=src), reads=[srcb], writes=[voutb])
            kb.op("dve", lambda: nc.vector.match_replace(out=tmp[:, 0:n], in_to_replace=vout[:, 0:8],
                                                         in_values=src, imm_value=NEG),
                  reads=[srcb, voutb], writes=[tmpb])
            kb.op("dve", lambda: nc.vector.max(out=vout[:, 8:16], in_=tmp[:, 0:n]), reads=[tmpb], writes=[voutb])
            kb.op("dve", lambda: nc.vector.max_index(out=iout[:, 0:8], in_max=vout[:, 0:8], in_values=src),
                  reads=[srcb, voutb], writes=[ioutb])
            kb.op("dve", lambda: nc.vector.max_index(out=iout[:, 8:16], in_max=vout[:, 8:16], in_values=src),
                  reads=[srcb, voutb], writes=[ioutb])

        for t in range(NT):
            if t == 0:
                emit_mod(kb, cvl, ada_w, ada_b, 0, 4096, mod, modb)
                kb.ts(mod[:, 2048:3072], mod[:, 2048:3072], 1.0, None, ALU.add, reads=[modb], writes=[modb])
            elif t == NL:
                emit_mod(kb, cvc, ada_w, ada_b, 0, 4096, mod, modb)
                kb.ts(mod[:, 2048:3072], mod[:, 2048:3072], 1.0, None, ALU.add, reads=[modb], writes=[modb])
            kb.dma("sp", xt[:], x[t], writes=[xtb])
            if even:
                kb.dma("sp", mix[:], mixd[t], writes=[mixb])
                kb.dma("sp", gt[:, 0:512], gated[t], writes=[gtb])
                rv = mix[:, 512:1024].rearrange("p (h d) -> p h d", h=8)
                kb.red(sm[:, 0:8], rv, ALU.add, reads=[mixb], writes=[smb])
                kb.ts(sm[:, 0:8], sm[:, 0:8], 1.0 / 64, None, ALU.mult, reads=[smb], writes=[smb])
                kb.tt(rv, rv, sm[:, 0:8].unsqueeze(2).to_broadcast([128, 8, 64]), ALU.subtract,
                      reads=[mixb, smb], writes=[mixb])
                jv = junk[:, 0:512].rearrange("p (h d) -> p h d", h=8)
                kb.tt(jv, rv, rv, ALU.mult, reads=[mixb], writes=[junkb])
                kb.red(sm[:, 8:16], jv, ALU.add, reads=[junkb], writes=[smb])
                emit_rstd(kb, sm[:, 8:16], sm[:, 8:16], 1.0 / 64, 1e-5, reads=[smb], writes=[smb])
                kb.tt(rv, rv, sm[:, 8:16].unsqueeze(2).to_broadcast([128, 8, 64]), ALU.mult,
                      reads=[mixb, smb], writes=[mixb])
                kb.tt(mix[:, 512:1024], mix[:, 512:1024], gg[:], ALU.mult, reads=[mixb, ggb], writes=[mixb])
                kb.act(gt[:, 0:512], gt[:, 0:512], AF.Silu, reads=[gtb], writes=[gtb])
                kb.tt(mix[:, 512:1024], mix[:, 512:1024], gt[:, 0:512], ALU.mult, reads=[mixb, gtb], writes=[mixb])
            else:
                kb.dma("sp", mix[:], mixd[0, t], writes=[mixb])
                kb.dma("sp", mix2[:], mixd[1, t], writes=[mix2b])
                kb.dma("sp", gt[:], gated[t], writes=[gtb])
                kb.tt(mix[:], mix[:], mix2[:], ALU.add, reads=[mixb, mix2b], writes=[mixb])
                ov = mix[:].rearrange("p (h d) -> p h d", h=8)
                jv = junk[:].rearrange("p (h d) -> p h d", h=8)
                kb.tt(jv, ov, ov, ALU.mult, reads=[mixb], writes=[junkb])
                kb.red(sm[:, 0:8], jv, ALU.add, reads=[junkb], writes=[smb])
                emit_rstd(kb, sm[:, 0:8], sm[:, 0:8], 1.0 / 128, 1e-6, reads=[smb], writes=[smb])
                kb.tt(ov, ov, sm[:, 0:8].unsqueeze(2).to_broadcast([128, 8, 128]), ALU.mult,
                      reads=[mixb, smb], writes=[mixb])
                kb.tt(ov, ov, gg[:].unsqueeze(1).to_broadcast([128, 8, 128]), ALU.mult,
                      reads=[mixb, ggb], writes=[mixb])
                kb.act(gt[:], gt[:], AF.Silu, reads=[gtb], writes=[gtb])
                kb.tt(mix[:], mix[:], gt[:], ALU.mult, reads=[mixb, gtb], writes=[mixb])

            def transpose8(src, srcb):
                for half in range(2):
                    p, pb = pbig[half]
                    for k in range(4):
                        kk = half * 4 + k
                        kb.tr(p[:, k * 128:(k + 1) * 128], src[:, kk * 128:(kk + 1) * 128], ident,
                              reads=[srcb, csb], writes=[pb])
                    kb.copy(cT[:, half * 4:half * 4 + 4, :], p[:].rearrange("p (k n) -> p k n", k=4),
                            reads=[pb], writes=[cTb], e="act")
            transpose8(mix, mixb)
            for c in range(2):
                w, wb = wch[wi % 2]
                pp, ppb = pz[wi % 2]
                wi += 1
                kb.dma("sp", w[:], w_out[:, c * 512:(c + 1) * 512].rearrange("(k p) n -> p k n", p=128), writes=[wb])
                for k in range(8):
                    kb.mm(pp[:], cT[:, k, :], w[:, k, :], k == 0, k == 7, reads=[cTb, wb], writes=[ppb])
                kb.tt(tt_[:, c * 512:(c + 1) * 512], pp[:], mod[:, c * 512:(c + 1) * 512], ALU.mult,
                      reads=[ppb, modb], writes=[ttb])
            kb.stt(tt_[:], xt[:], ALPHA, tt_[:], ALU.mult, ALU.add, reads=[xtb, ttb], writes=[ttb])
            emit_ln(kb, x1, x1b, tt_, ttb, ln[:, 0, :], lnb, ln[:, 1, :], lnb, st6, st6b, mv, mvb)
            kb.tt(xb_[:], x1[:], mod[:, 2048:3072], ALU.mult, reads=[x1b, modb], writes=[xbb])
            kb.tt(xb_[:], xb_[:], mod[:, 1024:2048], ALU.add, reads=[xbb, modb], writes=[xbb])
            transpose8(xb_, xbb)
            for c in range(4):
                w, wb = wch[wi % 2]
                pp, ppb = pz[wi % 2]
                wi += 1
                kb.dma("sp", w[:], w_q[:, c * 512:(c + 1) * 512].rearrange("(k p) n -> p k n", p=128), writes=[wb])
                for j in range(4):
                    for k in range(8):
                        kb.mm(pp[:, j * 128:(j + 1) * 128], w[:, k, j * 128:(j + 1) * 128], cT[:, k, :],
                              k == 0, k == 7, reads=[cTb, wb], writes=[ppb])
                kb.copy(qT[:, c * 4:(c + 1) * 4, :], pp[:].rearrange("p (j n) -> p j n", j=4),
                        reads=[ppb], writes=[qTb], e="act")
            for c in range(4):
                pp, ppb = pbig[c]
                for j in range(4):
                    jj = c * 4 + j
                    kb.mm(pp[:, j * 128:(j + 1) * 128], qT[:, jj, :], kT[:, jj, :], True, True,
                          reads=[qTb, kTb], writes=[ppb])
                kb.copy(ssb[:, c * 4:(c + 1) * 4, :], pp[:].rearrange("p (j n) -> p j n", j=4),
                        reads=[ppb], writes=[ssbb], e=("act" if c % 2 else "dve"))
            for j in range(16):
                top16(ssb[:, j, :], ssbb, 128, vals[:, j, :], valsb, idxu[:, j, :], idxub)
            kb.copy(idxf[:].rearrange("p a b -> p (a b)"), idxu[:].rearrange("p a b -> p (a b)"),
                    reads=[idxub], writes=[idxfb])
            kb.memset(acc[:], 0.0, writes=[accb], e="pool")
            for h in range(8):
                v1 = vals[:, 2 * h, :]
                v2 = vals[:, 2 * h + 1, :]
                cv = cand[:].rearrange("p (a b) -> p a b", a=16)
                kb.tt(cv, v1.unsqueeze(2).to_broadcast([128, 16, 16]), v2.unsqueeze(1).to_broadcast([128, 16, 16]),
                      ALU.add, reads=[valsb], writes=[candb])
                top16(cand[:], candb, 256, sc16, sc16b, jpu, jpub)
                kb.ts(jau[:, 0, :], jpu[:], 4, None, ALU.logical_shift_right, reads=[jpub], writes=[jaub])
                kb.ts(jau[:, 1, :], jpu[:], 15, None, ALU.bitwise_and, reads=[jpub], writes=[jaub])
                kb.copy(jaf[:].rearrange("p a b -> p (a b)"), jau[:].rearrange("p a b -> p (a b)"),
                        reads=[jaub], writes=[jafb])
                for s_ in range(2):
                    kb.tt(oh[:], jaf[:, s_, :].unsqueeze(2).to_broadcast([128, 16, 16]),
                          iota.unsqueeze(1).to_broadcast([128, 16, 16]), ALU.is_equal,
                          reads=[jafb, csb], writes=[ohb])
                    kb.tt(oh[:], oh[:], idxf[:, 2 * h + s_, :].unsqueeze(1).to_broadcast([128, 16, 16]), ALU.mult,
                          reads=[ohb, idxfb], writes=[ohb])
                    kb.red(isel[:, s_, :], oh[:], ALU.add, reads=[ohb], writes=[iselb])
                kb.stt(ef[:], isel[:, 0, :], 128.0, isel[:, 1, :], ALU.mult, ALU.add, reads=[iselb], writes=[efb])
                e_i, e_ib = ei[h % 2]
                kb.copy(e_i[:], ef[:], reads=[efb], writes=[e_ib])
                kb.ts(sm[:, 0:1], sc16[:, 0:1], -1.0, None, ALU.mult, reads=[sc16b], writes=[smb])
                kb.act(ex[:], sc16[:], AF.Exp, reads=[sc16b, smb], writes=[exb, smb], bias=sm[:, 0:1], scale=1.0,
                       accum_out=sm[:, 1:2])
                kb.op("dve", lambda: nc.vector.reciprocal(out=sm[:, 1:2], in_=sm[:, 1:2]), reads=[smb], writes=[smb])
                for k in range(16):
                    u, ubb = ub[k % 2]
                    kb.dma("pool", None, None, reads=[e_ib], writes=[ubb],
                           fn=lambda: nc.gpsimd.indirect_dma_start(
                               out=u[:], out_offset=None, in_=u_tab,
                               in_offset=bass.IndirectOffsetOnAxis(ap=e_i[:, k:k + 1], axis=0)))
                    kb.stt(junk[:], u[:], 1.0, xb_[:], ALU.mult, ALU.mult, reads=[ubb, xbb], writes=[junkb, dotsb],
                           accum_out=dots[:, k:k + 1])
                w_, w_b = wg[h % 2]
                kb.act(w_[:], dots[:], AF.Gelu, reads=[dotsb], writes=[w_b])
                kb.tt(w_[:], w_[:], ex[:], ALU.mult, reads=[w_b, exb], writes=[w_b])
                kb.ts(w_[:], w_[:], sm[:, 1:2], None, ALU.mult, reads=[w_b, smb], writes=[w_b])
                for k in range(16):
                    v, vbb = vb[k % 2]
                    kb.dma("pool", None, None, reads=[e_ib], writes=[vbb],
                           fn=lambda: nc.gpsimd.indirect_dma_start(
                               out=v[:], out_offset=None, in_=v_tab,
                               in_offset=bass.IndirectOffsetOnAxis(ap=e_i[:, k:k + 1], axis=0)))
                    kb.stt(acc[:], v[:], w_[:, k:k + 1], acc[:], ALU.mult, ALU.add, reads=[vbb, w_b, accb], writes=[accb])
            kb.tt(acc[:], acc[:], mod[:, 3072:4096], ALU.mult, reads=[accb, modb], writes=[accb])
            kb.stt(acc[:], x1[:], ALPHA, acc[:], ALU.mult, ALU.add, reads=[x1b, accb], writes=[accb])
            emit_ln(kb, tt_, ttb, acc, accb, ln[:, 2, :], lnb, ln[:, 3, :], lnb, st6, st6b, mv, mvb)
            kb.dma("sp", out[t], tt_[:], reads=[ttb], writes=[obuf], nowaw=True)
        kb.finish([obuf])
    return nc


SEQ = 8192
CTX = 256
BATCH = 4
NT_TOK = 33
NKT_ALL = 66
NCT_ALL = 2
_PROGS = {}


def _prog(key, fn):
    if key not in _PROGS:
        _PROGS[key] = fn()
    return _PROGS[key]


def _run(nc, in_maps):
    import time as _t
    t0 = _t.time()
    in_maps = [{k: np.ascontiguousarray(v, dtype=np.float32) for k, v in m.items()} for m in in_maps]
    nb = sum(v.nbytes for m in in_maps for v in m.values())
    res = run_bass_kernel_spmd(nc, in_maps, core_ids=list(range(len(in_maps))))
    print("[launch] in_bytes=%.1fMB  %.1fs" % (nb / 1e6, _t.time() - t0), flush=True)
    return res.results


def _rope_tables(dim):
    n_freq = dim // 4
    inv = 10000.0 ** (-np.arange(n_freq, dtype=np.float64) / n_freq)
    pos = np.arange(SEQ)
    row = (pos // 64).astype(np.float64)
    colp = (pos % 64).astype(np.float64)
    ang = np.concatenate([row[:, None] * inv, colp[:, None] * inv], axis=-1)
    ang = ang.astype(np.float32).astype(np.float64)
    return np.concatenate([np.cos(ang), np.sin(ang)], -1).astype(np.float32)


def _tok_tiles(xl, xc, core):
    b, half = core // 2, core % 2
    lat = xl[b, half * 4096:(half + 1) * 4096].reshape(32, 128, -1)
    ctx = xc[b, half * 128:(half + 1) * 128].reshape(1, 128, -1)
    return np.concatenate([lat, ctx], 0)


def _untile(outs, width):
    lat = np.zeros((BATCH, SEQ, width), np.float32)
    ctx = np.zeros((BATCH, CTX, width), np.float32)
    for core in range(NCORES):
        b, half = core // 2, core % 2
        o = outs[core]
        lat[b, half * 4096:(half + 1) * 4096] = o[:32].reshape(4096, width)
        ctx[b, half * 128:(half + 1) * 128] = o[32]
    return lat, ctx


def _colT(v):
    return np.ascontiguousarray(v.reshape(8, 128).T)


def kernel(x, c, ctx, c_ctx, ada_w, ada_b, ln1_g, ln1_b, ln2_g, ln2_b,
           ar_w_in, mla_q_norm, mla_w_uq, mla_kv_norm, mla_w_ukv, ret_gn_g, ar_w_out,
           dn_w_in, dn_conv, dn_a_log, dn_dt_bias, dn_norm_g, dn_w_out,
           peer_w_q, peer_k1, peer_k2, peer_u, peer_v):
    f32 = np.float32
    xl = np.asarray(x, f32)
    xc = np.asarray(ctx, f32)
    ident = np.eye(128, dtype=f32)
    ropem_full = _rope_tables(32)
    roper_full = _rope_tables(64)
    one0 = lambda n: np.concatenate([np.ones((128, n), f32), np.zeros((128, n), f32)], -1)
    cstC = np.zeros((128, 144), f32)
    cstC[:, :128] = ident
    cstC[:, 128:] = np.arange(16)[None]
    NK = NKT_ALL * 128
    for l in range(DEPTH):
        even = (l % 2 == 0)
        j = l // 2
        ncA = _prog(("A", even), lambda: build_A(even, NT_TOK, 32))
        maps = []
        for core in range(NCORES):
            b, half = core // 2, core % 2
            m = dict(x=_tok_tiles(xl, xc, core), cvl=_colT(np.asarray(c[b], f32)), cvc=_colT(np.asarray(c_ctx, f32)),
                     ada_w=ada_w[l][:, 0:2048], ada_b=ada_b[l][0:2048], ident=ident)
            if even:
                sl = slice(half * 4096, (half + 1) * 4096)
                m.update(w_in=ar_w_in[j], q_norm=mla_q_norm[j], w_uq=mla_w_uq[j], kv_norm=mla_kv_norm[j],
                         w_ukv=mla_w_ukv[j],
                         ropem=np.concatenate([ropem_full[sl].reshape(32, 128, 32), one0(16)[None]], 0),
                         roper=np.concatenate([roper_full[sl].reshape(32, 128, 64), one0(32)[None]], 0))
            else:
                m.update(w_in=dn_w_in[j], a_log=np.asarray(dn_a_log[j], f32).reshape(16),
                         dt_bias=np.asarray(dn_dt_bias[j], f32).reshape(16))
            maps.append(m)
        resA = _run(ncA, maps)
        W = A_EVEN_W if even else A_ODD_W
        zl, zc = _untile([r["out"] for r in resA], W)
        seq = np.concatenate([zc, zl], 1)
        del resA
        if even:
            ncM = _prog("ME", lambda: build_ME(NKT_ALL, NCT_ALL, 4))
            SEL = np.zeros((65, 64), f32)
            SEL[64] = 1
            pos = np.arange(128, dtype=np.float64)
            maps = []
            for core in range(NCORES):
                b, hh = core // 2, core % 2
                heads = np.arange(4) + hh * 4
                s = seq[b]
                q = s[:, 0:768].reshape(NK, 8, 96)
                kv = s[:, 768:1792].reshape(NK, 8, 128)
                kr = s[:, 1792:1824]
                QT = np.zeros((4, 128, NK), f32)
                KT = np.zeros((4, 128, NK), f32)
                VA = np.ones((4, 128, NKT_ALL, 65), f32)
                KN2 = np.zeros((4, 1, NK), f32)
                tokm = lambda a: a.reshape(NKT_ALL, 128, 64).transpose(1, 0, 2)
                rq = s[:, 1840:2352].reshape(NK, 8, 64)
                rk = s[:, 2352:2864].reshape(NK, 8, 64)
                rv = s[:, 2864:3376].reshape(NK, 8, 64)
                RQT = np.zeros((4, 64, NK), f32)
                RKT = np.zeros((4, 64, NK), f32)
                RK = np.zeros((4, 128, NKT_ALL, 64), f32)
                RV = np.zeros((4, 128, NKT_ALL, 64), f32)
                for i, h in enumerate(heads):
                    QT[i, 32:128] = q[:, h].T
                    QT[i, 0] = s[:, 1824 + h]
                    KT[i, 32:96] = kv[:, h, :64].T
                    KT[i, 96:128] = kr.T
                    KT[i, 0] = 1.0
                    VA[i, :, :, :64] = tokm(kv[:, h, 64:])
                    KN2[i, 0] = s[:, 1832 + h]
                    RQT[i] = rq[:, h].T
                    RKT[i] = rk[:, h].T
                    RK[i] = tokm(rk[:, h])
                    RV[i] = tokm(rv[:, h])
                gam = 1.0 - 2.0 ** (-5.0 - heads.astype(np.float64))
                lg = np.log1p(-2.0 ** (-5.0 - heads.astype(np.float64)))
                DSYM = np.exp(lg[:, None, None] * np.abs(pos[:, None] - pos[None, :])[None]).astype(f32)
                DCOL = np.stack([np.exp(lg[None] * (127 - pos[:, None])), np.exp(lg[None] * pos[:, None]),
                                 np.exp(lg[None] * (pos[:, None] + 1)), np.exp(lg[None] * (128 - pos[:, None]))],
                                -1).astype(f32)
                GC = np.broadcast_to(np.exp(lg * 128)[None, :], (64, 4)).astype(f32)
                maps.append(dict(QT=QT, KT=KT, VA=VA, KN2=KN2, SEL=SEL, RQT=RQT, RKT=RKT, RK=RK, RV=RV,
                                 DSYM=DSYM, DCOL=DCOL, GC=GC))
            resM = _run(ncM, maps)
            mix = np.zeros((BATCH, NK, 1024), f32)
            for core in range(NCORES):
                b, hh = core // 2, core % 2
                r = resM[core]
                for i in range(4):
                    h = hh * 4 + i
                    mix[b, :, h * 64:(h + 1) * 64] = r["OT"][i].T
                    mix[b, :, 512 + h * 64:512 + (h + 1) * 64] = r["RO"][i].transpose(1, 0, 2).reshape(NK, 64)
            gate = seq[:, :, 3376:3888]
            del resM, maps
            mixc, mixl = mix[:, :CTX], mix[:, CTX:]
        else:
            ncM = _prog("MO", lambda: build_MO(NKT_ALL, NCT_ALL, 4))
            CM = mo_consts()
            maps = []
            for core in range(NCORES):
                b, hh = core // 2, core % 2
                s = seq[b]
                PZ = np.zeros((3, 4, 128, NK + 8), f32)
                CWt = np.zeros((4, 128, 3, 5), f32)
                GG = np.zeros((128, 2, NKT_ALL, 4), f32)
                BB = np.zeros((128, 2, NKT_ALL, 4), f32)
                for i in range(4):
                    h = hh * 4 + i
                    for w in range(3):
                        cols = slice(w * 1024 + h * 128, w * 1024 + (h + 1) * 128)
                        zz = s[:, cols].T
                        PZ[w, i, :, 2:2 + CTX] = zz[:, :CTX]
                        PZ[w, i, :, CTX + 6:CTX + 6 + SEQ] = zz[:, CTX:]
                        CWt[i, :, w, :] = np.asarray(dn_conv[j], f32)[:, cols].T
                    for d in range(2):
                        GG[:, d, :, i] = s[:, 4096 + d * 8 + h].reshape(NKT_ALL, 128).T
                        BB[:, d, :, i] = s[:, 4112 + d * 8 + h].reshape(NKT_ALL, 128).T
                maps.append(dict(PZ=PZ, CW=CWt, GG=GG, BB=BB, CM=CM))
            resM = _run(ncM, maps)
            mix2 = np.zeros((2, BATCH, NK, 1024), f32)
            for core in range(NCORES):
                b, hh = core // 2, core % 2
                o = resM[core]["OUT"]
                for i in range(4):
                    h = hh * 4 + i
                    for d in range(2):
                        mix2[d, b, :, h * 128:(h + 1) * 128] = o[d, i].transpose(1, 0, 2).reshape(NK, 128)
            gate = seq[:, :, 3072:4096]
            del resM, maps
        gatec, gatel = gate[:, :CTX], gate[:, CTX:]
        ncC = _prog(("C", even), lambda: build_C(even, NT_TOK, 32))
        kT = np.zeros((128, 16, 128), f32)
        for h in range(8):
            kT[:, 2 * h, :] = np.asarray(peer_k1[l][h], f32).T
            kT[:, 2 * h + 1, :] = np.asarray(peer_k2[l][h], f32).T
        lnp = np.stack([ln1_g[l], ln1_b[l], ln2_g[l], ln2_b[l]]).astype(f32)
        maps = []
        for core in range(NCORES):
            b, half = core // 2, core % 2
            m = dict(x=_tok_tiles(xl, xc, core), cvl=_colT(np.asarray(c[b], f32)), cvc=_colT(np.asarray(c_ctx, f32)),
                     ada_w=ada_w[l][:, 2048:6144], ada_b=ada_b[l][2048:6144], lnp=lnp, w_q=peer_w_q[l], kT=kT,
                     u_tab=peer_u[l], v_tab=peer_v[l], cst=cstC, gate=_tok_tiles(gatel, gatec, core))
            if even:
                m.update(mix=_tok_tiles(mixl, mixc, core), gn_g=ret_gn_g[j], w_out=ar_w_out[j])
            else:
                m.update(mix=np.stack([_tok_tiles(mix2[d][:, CTX:], mix2[d][:, :CTX], core) for d in range(2)]),
                         gn_g=dn_norm_g[j], w_out=dn_w_out[j])
            maps.append(m)
        resC = _run(ncC, maps)
        xl, xc = _untile([r["out"] for r in resC], 1024)
        del resC, maps
    return xl
```

```python
import math
from contextlib import ExitStack
import numpy as np
import concourse.bass as bass
import concourse.mybir as mybir
from concourse.bass_utils import run_bass_kernel_spmd

F32 = mybir.dt.float32
I32 = mybir.dt.int32
U32 = mybir.dt.uint32
AF = mybir.ActivationFunctionType
ALU = mybir.AluOpType
AX = mybir.AxisListType

NCORES = 8
D = 1024
DEPTH = 4
ALPHA = (2 * DEPTH) ** 0.25


class Buf:
    __slots__ = ("name", "w", "r", "dsem", "excl")

    def __init__(self, name="b", excl=False):
        self.name = name
        self.w = None
        self.r = []
        self.dsem = None
        self.excl = excl


class KB:
    ENG = ("pe", "act", "dve", "pool", "sp")

    def __init__(self, nc, stack):
        self.nc = nc
        self.stack = stack
        self.eng = {"pe": nc.tensor, "act": nc.scalar, "dve": nc.vector,
                    "pool": nc.gpsimd, "sp": nc.sync}
        self.sem = {}
        self.cnt = {}
        for e in self.ENG:
            self.sem[e] = stack.enter_context(nc.semaphore("s_" + e))
            self.cnt[e] = 0
        self.waited = {e: {} for e in self.ENG}
        self.semobj = dict(self.sem)
        self.dmacnt = {}
        self.nd = 0
        self.ninst = 0
        self.nt = 0

    def sb(self, shape, dt=F32, name=None):
        self.nt += 1
        t = self.stack.enter_context(self.nc.sbuf_tensor(name or ("t%d" % self.nt), list(shape), dt))
        return t, Buf(name or ("t%d" % self.nt))

    def ps(self, shape, dt=F32, name=None):
        self.nt += 1
        t = self.stack.enter_context(self.nc.psum_tensor(name or ("p%d" % self.nt), list(shape), dt))
        return t, Buf(name or ("p%d" % self.nt), excl=True)

    def _dsem(self, b):
        if b.dsem is None:
            key = "d%d" % self.nd
            self.nd += 1
            h = self.stack.enter_context(self.nc.semaphore(key))
            self.semobj[key] = h
            self.dmacnt[key] = 0
            b.dsem = key
        return b.dsem

    def _need(self, e, deps):
        eng = self.eng[e]
        best = {}
        for d in deps:
            if d is None:
                continue
            k, v = d
            if k in self.dmacnt:
                v = self.dmacnt[k]
            if e == "pe" and k == "pe":
                continue
            if best.get(k, 0) < v:
                best[k] = v
        for k, v in best.items():
            if self.waited[e].get(k, 0) < v:
                eng.wait_ge(self.semobj[k], v)
                self.waited[e][k] = v
                self.ninst += 1

    @staticmethod
    def _deps(reads, writes):
        deps = []
        for b in reads:
            deps.append(b.w)
        for b in writes:
            deps.append(b.w)
            deps.extend(b.r)
        return deps

    def op(self, e, fn, reads=(), writes=()):
        rx = [b for b in reads if b.excl]
        if rx:
            writes = list(writes) + rx
        self._need(e, self._deps(reads, writes))
        inst = fn()
        self.cnt[e] += 1
        inst.then_inc(self.sem[e], 1)
        tok = (e, self.cnt[e])
        for b in reads:
            b.r.append(tok)
        for b in writes:
            b.w = tok
            b.r = []
        self.ninst += 1
        return inst

    def dma(self, q, out, in_, reads=(), writes=(), fn=None, nowaw=False, **kw):
        self._need(q, self._deps(reads, () if nowaw else writes))
        key = self._dsem(writes[0])
        if fn is None:
            inst = self.eng[q].dma_start(out=out, in_=in_, **kw)
        else:
            inst = fn()
        inst.then_inc(self.semobj[key], 16)
        self.dmacnt[key] += 16
        tok = (key, self.dmacnt[key])
        for b in reads:
            b.r.append(tok)
        for b in writes:
            b.w = tok
            b.r = []
        self.ninst += 1
        return inst

    def finish(self, bufs):
        allc = [(e, self.cnt[e]) for e in self.ENG if self.cnt[e] > 0]
        for e in ("sp", "pool", "act"):
            self._need(e, [b.w for b in bufs] + [c for c in allc if c[0] != e])

    def mm(self, out, lhsT, rhs, start, stop, reads, writes):
        nc = self.nc
        return self.op("pe", lambda: nc.tensor.matmul(out, lhsT, rhs, start=start, stop=stop),
                       reads=reads, writes=writes)

    def tr(self, out, in_, ident, reads, writes):
        nc = self.nc
        return self.op("pe", lambda: nc.tensor.transpose(out, in_, ident), reads=reads, writes=writes)

    def act(self, out, in_, func, reads, writes, e="act", **kw):
        nc = self.nc
        return self.op("act", lambda: nc.scalar.activation(out=out, in_=in_, func=func, **kw),
                       reads=reads, writes=writes)

    def tt(self, out, in0, in1, op, reads, writes, e="dve"):
        eng = self.eng[e]
        return self.op(e, lambda: eng.tensor_tensor(out=out, in0=in0, in1=in1, op=op),
                       reads=reads, writes=writes)

    def ts(self, out, in0, s1, s2, op0, op1=None, reads=(), writes=(), e="dve", **kw):
        eng = self.eng[e]
        if op1 is None:
            return self.op(e, lambda: eng.tensor_scalar(out=out, in0=in0, scalar1=s1, scalar2=None,
                                                        op0=op0, **kw), reads=reads, writes=writes)
        return self.op(e, lambda: eng.tensor_scalar(out=out, in0=in0, scalar1=s1, scalar2=s2,
                                                    op0=op0, op1=op1, **kw), reads=reads, writes=writes)

    def stt(self, out, in0, scalar, in1, op0, op1, reads, writes, **kw):
        nc = self.nc
        return self.op("dve", lambda: nc.vector.scalar_tensor_tensor(out=out, in0=in0, scalar=scalar, in1=in1,
                                                                     op0=op0, op1=op1, **kw),
                       reads=reads, writes=writes)

    def copy(self, out, in_, reads, writes, e="dve"):
        if e == "act":
            return self.act(out, in_, AF.Copy, reads, writes)
        eng = self.eng[e]
        return self.op(e, lambda: eng.tensor_copy(out=out, in_=in_), reads=reads, writes=writes)

    def red(self, out, in_, op, reads, writes, axis=None, e="dve"):
        eng = self.eng[e]
        ax = axis if axis is not None else AX.X
        return self.op(e, lambda: eng.tensor_reduce(out=out, in_=in_, axis=ax, op=op),
                       reads=reads, writes=writes)

    def memset(self, ap, val, writes, e="dve"):
        eng = self.eng[e]
        return self.op(e, lambda: eng.memset(ap, val), writes=writes)

    def bload(self, q, tile, row_ap, n, writes):
        return self.dma(q, tile, row_ap.partition_broadcast(128), writes=writes)


def emit_mod(kb, cvT, ada_w, ada_b, col0, ncols, out_t, out_b, ident_unused=None):
    nc = kb.nc
    cs, cs_b = kb.sb([128, 8])
    crep, crep_b = kb.sb([128, 8, 128])
    kb.dma("sp", cs[:], cvT, writes=[cs_b])
    kb.act(cs[:], cs[:], AF.Silu, reads=[cs_b], writes=[cs_b])
    for k in range(8):
        kb.copy(crep[:, k, :], cs[:, k:k + 1].to_broadcast([128, 128]), reads=[cs_b], writes=[crep_b])
    wch, wch_b = kb.sb([128, 8, 512])
    bb, bb_b = kb.sb([128, 512])
    pm, pm_b = kb.ps([128, 512])
    for c0 in range(0, ncols, 512):
        kb.dma("sp", wch[:], ada_w[:, col0 + c0:col0 + c0 + 512].rearrange("(k p) n -> p k n", p=128),
               writes=[wch_b])
        kb.bload("sp", bb[:], ada_b[col0 + c0:col0 + c0 + 512], 512, writes=[bb_b])
        for k in range(8):
            kb.mm(pm[:], crep[:, k, :], wch[:, k, :], k == 0, k == 7, reads=[crep_b, wch_b], writes=[pm_b])
        kb.tt(out_t[:, c0:c0 + 512], pm[:], bb[:], ALU.add, reads=[pm_b, bb_b], writes=[out_b])


def emit_rope(kb, x, xb, H, Dh, cs, sn, csb, tmp, tmpb):
    h2 = Dh // 2
    x1 = x[:, :, 0:h2]
    x2 = x[:, :, h2:Dh]
    cb = cs.unsqueeze(1).to_broadcast([128, H, h2])
    sb_ = sn.unsqueeze(1).to_broadcast([128, H, h2])
    t1 = tmp[:, 0, 0:H, 0:h2]
    t2 = tmp[:, 1, 0:H, 0:h2]
    t3 = tmp[:, 2, 0:H, 0:h2]
    kb.tt(t1, x1, sb_, ALU.mult, reads=[xb, csb], writes=[tmpb])
    kb.tt(t2, x2, sb_, ALU.mult, reads=[xb, csb], writes=[tmpb])
    kb.tt(x1, x1, cb, ALU.mult, reads=[xb, csb], writes=[xb])
    kb.tt(x2, x2, cb, ALU.mult, reads=[xb, csb], writes=[xb])
    kb.tt(x1, x1, t2, ALU.subtract, reads=[xb, tmpb], writes=[xb])
    kb.tt(x2, x2, t1, ALU.add, reads=[xb, tmpb], writes=[xb])


def emit_rstd(kb, out, in_, scale, eps, reads, writes):
    kb.ts(out, in_, scale, eps, ALU.mult, ALU.add, reads=reads, writes=writes)
    kb.act(out, out, AF.Sqrt, reads=writes, writes=writes)
    nc = kb.nc
    kb.op("dve", lambda: nc.vector.reciprocal(out=out, in_=out), reads=writes, writes=writes)


A_EVEN_W = 768 + 1024 + 32 + 8 + 8 + 512 * 4
A_ODD_W = 4096 + 16 + 16


def build_A(even, NT, NL):
    nc = bass.Bass("TRN2", target_bir_lowering=False)
    dr = lambda n, s, k="ExternalInput": nc.dram_tensor(n, list(s), F32, kind=k).ap()
    NIN = 2464 if even else 4128
    WOUT = A_EVEN_W if even else A_ODD_W
    x = dr("x", [NT, 128, D])
    cvl = dr("cvl", [128, 8])
    cvc = dr("cvc", [128, 8])
    ada_w = dr("ada_w", [D, 2048])
    ada_b = dr("ada_b", [2048])
    w_in = dr("w_in", [D, NIN])
    ident_d = dr("ident", [128, 128])
    if even:
        qn_d = dr("q_norm", [256])
        wuq_d = dr("w_uq", [256, 768])
        kvn_d = dr("kv_norm", [128])
        wukv_d = dr("w_ukv", [128, 1024])
        ropem = dr("ropem", [NT, 128, 32])
        roper = dr("roper", [NT, 128, 64])
    else:
        alog_d = dr("a_log", [16])
        dtb_d = dr("dt_bias", [16])
    out = dr("out", [NT, 128, WOUT], "ExternalOutput")

    with ExitStack() as st:
        kb = KB(nc, st)
        ident, ident_b = kb.sb([128, 128])
        kb.dma("sp", ident[:], ident_d, writes=[ident_b])
        modl, modl_b = kb.sb([128, 2048])
        modc, modc_b = kb.sb([128, 2048])
        emit_mod(kb, cvl, ada_w, ada_b, 0, 2048, modl, modl_b)
        if NT > NL:
            emit_mod(kb, cvc, ada_w, ada_b, 0, 2048, modc, modc_b)
        kb.ts(modl[:, 1024:2048], modl[:, 1024:2048], 1.0, None, ALU.add, reads=[modl_b], writes=[modl_b])
        if NT > NL:
            kb.ts(modc[:, 1024:2048], modc[:, 1024:2048], 1.0, None, ALU.add, reads=[modc_b], writes=[modc_b])

        xt, xt_b = kb.sb([128, D])
        hT, hT_b = kb.sb([128, 8, 128])
        z, z_b = kb.sb([128, WOUT])
        wch = [kb.sb([128, 8, 512]) for _ in range(2)]
        pT = [kb.ps([128, 512]) for _ in range(2)]
        pz = [kb.ps([128, 512]) for _ in range(2)]
        if even:
            qn, qn_b = kb.sb([128, 256])
            kvn, kvn_b = kb.sb([128, 128])
            kb.bload("sp", qn[:], qn_d, 256, writes=[qn_b])
            kb.bload("sp", kvn[:], kvn_d, 128, writes=[kvn_b])
            wuq, wuq_b = kb.sb([128, 2, 768])
            wukv, wukv_b = kb.sb([128, 1024])
            kb.dma("sp", wuq[:], wuq_d.rearrange("(k p) n -> p k n", p=128), writes=[wuq_b])
            kb.dma("sp", wukv[:], wukv_d, writes=[wukv_b])
            zin, zin_b = kb.sb([128, 2464])
            rm, rm_b = kb.sb([128, 32])
            rr, rr_b = kb.sb([128, 64])
            tmp, tmp_b = kb.sb([128, 3, 8, 32])
            sm, sm_b = kb.sb([128, 8])
            cn, cn_b = kb.sb([128, 256])
            cnT, cnT_b = kb.sb([128, 3, 128])
            sq, sq_b = kb.sb([128, 1024])
            pq = [kb.ps([128, 512]) for _ in range(2)]
        else:
            alog, alog_b = kb.sb([128, 16])
            dtb, dtb_b = kb.sb([128, 16])
            kb.bload("sp", alog[:], alog_d, 16, writes=[alog_b])
            kb.bload("sp", dtb[:], dtb_d, 16, writes=[dtb_b])
            kb.act(alog[:], alog[:], AF.Exp, reads=[alog_b], writes=[alog_b])
            kb.ts(alog[:], alog[:], -1.0, None, ALU.mult, reads=[alog_b], writes=[alog_b])
            t16 = [kb.sb([128, 16]) for _ in range(3)]

        ncol_chunks = [(c, min(512, NIN - c)) for c in range(0, NIN, 512)]
        wi = 0
        for t in range(NT):
            mod, mod_b = (modl, modl_b) if t < NL else (modc, modc_b)
            kb.dma("sp", xt[:], x[t], writes=[xt_b])
            kb.tt(xt[:], xt[:], mod[:, 1024:2048], ALU.mult, reads=[xt_b, mod_b], writes=[xt_b])
            kb.tt(xt[:], xt[:], mod[:, 0:1024], ALU.add, reads=[xt_b, mod_b], writes=[xt_b])
            for half in range(2):
                p, p_b = pT[half]
                for k in range(4):
                    kk = half * 4 + k
                    kb.tr(p[:, k * 128:(k + 1) * 128], xt[:, kk * 128:(kk + 1) * 128], ident[:],
                          reads=[xt_b, ident_b], writes=[p_b])
                kb.copy(hT[:, half * 4:half * 4 + 4, :], p[:].rearrange("p (k n) -> p k n", k=4),
                        reads=[p_b], writes=[hT_b], e="act")
            zdst, zdst_b = (zin, zin_b) if even else (z, z_b)
            for (c0, cw) in ncol_chunks:
                w, w_b = wch[wi % 2]
                pp, pp_b = pz[wi % 2]
                wi += 1
                kb.dma("sp", w[:, :, 0:cw], w_in[:, c0:c0 + cw].rearrange("(k p) n -> p k n", p=128),
                       writes=[w_b])
                for k in range(8):
                    kb.mm(pp[:, 0:cw], hT[:, k, :], w[:, k, 0:cw], k == 0, k == 7,
                          reads=[hT_b, w_b], writes=[pp_b])
                kb.copy(zdst[:, c0:c0 + cw], pp[:, 0:cw], reads=[pp_b], writes=[zdst_b],
                        e=("act" if (wi % 2) else "dve"))
            if even:
                kb.dma("sp", rm[:], ropem[t], writes=[rm_b])
                kb.dma("sp", rr[:], roper[t], writes=[rr_b])
                kb.act(sq[:, 0:256], zin[:, 0:256], AF.Square, reads=[zin_b], writes=[sq_b, sm_b],
                       accum_out=sm[:, 0:1])
                kb.act(sq[:, 0:128], zin[:, 256:384], AF.Square, reads=[zin_b], writes=[sq_b, sm_b],
                       accum_out=sm[:, 1:2])
                emit_rstd(kb, sm[:, 0:1], sm[:, 0:1], 1.0 / 256, 1e-6, reads=[sm_b], writes=[sm_b])
                emit_rstd(kb, sm[:, 1:2], sm[:, 1:2], 1.0 / 128, 1e-6, reads=[sm_b], writes=[sm_b])
                kb.stt(cn[:, 0:256], zin[:, 0:256], sm[:, 0:1], qn[:], ALU.mult, ALU.mult,
                       reads=[zin_b, sm_b, qn_b], writes=[cn_b])
                p, p_b = pT[0]
                for k in range(2):
                    kb.tr(p[:, k * 128:(k + 1) * 128], cn[:, k * 128:(k + 1) * 128], ident[:],
                          reads=[cn_b, ident_b], writes=[p_b])
                kb.copy(cnT[:, 0:2, :], p[:, 0:256].rearrange("p (k n) -> p k n", k=2),
                        reads=[p_b], writes=[cnT_b], e="act")
                kb.stt(cn[:, 0:128], zin[:, 256:384], sm[:, 1:2], kvn[:], ALU.mult, ALU.mult,
                       reads=[zin_b, sm_b, kvn_b], writes=[cn_b])
                p, p_b = pT[1]
                kb.tr(p[:, 0:128], cn[:, 0:128], ident[:], reads=[cn_b, ident_b], writes=[p_b])
                kb.copy(cnT[:, 2, :], p[:, 0:128], reads=[p_b], writes=[cnT_b], e="act")
                for (c0, cw) in ((0, 512), (512, 256)):
                    pp, pp_b = pq[0] if c0 == 0 else pq[1]
                    for k in range(2):
                        kb.mm(pp[:, 0:cw], cnT[:, k, :], wuq[:, k, c0:c0 + cw], k == 0, k == 1,
                              reads=[cnT_b, wuq_b], writes=[pp_b])
                    kb.copy(z[:, c0:c0 + cw], pp[:, 0:cw], reads=[pp_b], writes=[z_b])
                for j in range(2):
                    pp, pp_b = pq[j]
                    kb.mm(pp[:], cnT[:, 2, :], wukv[:, j * 512:(j + 1) * 512], True, True,
                          reads=[cnT_b, wukv_b], writes=[pp_b])
                    kb.copy(z[:, 768 + j * 512:768 + (j + 1) * 512], pp[:], reads=[pp_b], writes=[z_b], e="act")
                kb.copy(z[:, 1792:1824], zin[:, 384:416], reads=[zin_b], writes=[z_b])
                kb.copy(z[:, 1840:3888], zin[:, 416:2464], reads=[zin_b], writes=[z_b], e="pool")
                qv = z[:, 0:768].rearrange("p (h d) -> p h d", h=8)
                emit_rope(kb, qv[:, :, 64:96], z_b, 8, 32, rm[:, 0:16], rm[:, 16:32], rm_b, tmp, tmp_b)
                krv = z[:, 1792:1824].rearrange("p (h d) -> p h d", h=1)
                emit_rope(kb, krv, z_b, 1, 32, rm[:, 0:16], rm[:, 16:32], rm_b, tmp, tmp_b)
                rqv = z[:, 1840:2352].rearrange("p (h d) -> p h d", h=8)
                emit_rope(kb, rqv, z_b, 8, 64, rr[:, 0:32], rr[:, 32:64], rr_b, tmp, tmp_b)
                rkv = z[:, 2352:2864].rearrange("p (h d) -> p h d", h=8)
                emit_rope(kb, rkv, z_b, 8, 64, rr[:, 0:32], rr[:, 32:64], rr_b, tmp, tmp_b)
                kb.ts(z[:, 2352:2864], z[:, 2352:2864], 0.125, None, ALU.mult, reads=[z_b], writes=[z_b])
                kb.tt(sq[:, 0:768], z[:, 0:768], z[:, 0:768], ALU.mult, reads=[z_b], writes=[sq_b])
                kb.red(z[:, 1824:1832], sq[:, 0:768].rearrange("p (h d) -> p h d", h=8), ALU.add,
                       reads=[sq_b], writes=[z_b])
                kvv = z[:, 768:1792].rearrange("p (h d) -> p h d", h=8)
                sqv = sq[:, 0:512].rearrange("p (h d) -> p h d", h=8)
                kb.tt(sqv, kvv[:, :, 0:64], kvv[:, :, 0:64], ALU.mult, reads=[z_b], writes=[sq_b])
                kb.red(z[:, 1832:1840], sqv, ALU.add, reads=[sq_b], writes=[z_b])
                kb.tt(sq[:, 0:32], z[:, 1792:1824], z[:, 1792:1824], ALU.mult, reads=[z_b], writes=[sq_b])
                kb.red(sm[:, 2:3], sq[:, 0:32], ALU.add, reads=[sq_b], writes=[sm_b])
                kb.ts(z[:, 1832:1840], z[:, 1832:1840], sm[:, 2:3], None, ALU.add, reads=[z_b, sm_b], writes=[z_b])
            else:
                (xa, xa_b), (ta, ta_b), (tb, tb_b) = t16
                kb.tt(xa[:], z[:, 4096:4112], dtb[:], ALU.add, reads=[z_b, dtb_b], writes=[xa_b])
                kb.ts(ta[:], xa[:], -1.0, None, ALU.mult, reads=[xa_b], writes=[ta_b])
                kb.tt(ta[:], ta[:], xa[:], ALU.max, reads=[xa_b, ta_b], writes=[ta_b])
                kb.act(ta[:], ta[:], AF.Exp, reads=[ta_b], writes=[ta_b], scale=-1.0)
                kb.ts(ta[:], ta[:], 1.0, None, ALU.add, reads=[ta_b], writes=[ta_b])
                kb.act(ta[:], ta[:], AF.Ln, reads=[ta_b], writes=[ta_b])
                kb.ts(xa[:], xa[:], 0.0, None, ALU.max, reads=[xa_b], writes=[xa_b])
                kb.tt(xa[:], xa[:], ta[:], ALU.add, reads=[xa_b, ta_b], writes=[xa_b])
                kb.act(tb[:], z[:, 4112:4128], AF.Sigmoid, reads=[z_b], writes=[tb_b])
                kb.tt(z[:, 4096:4112], xa[:], alog[:], ALU.mult, reads=[xa_b, alog_b], writes=[z_b])
                kb.copy(z[:, 4112:4128], tb[:], reads=[tb_b], writes=[z_b])
            if t == 0:
                ob = Buf("out")
            kb.dma("sp", out[t], z[:], reads=[z_b], writes=[ob], nowaw=True)
        kb.finish([ob])
    return nc


MLA_SCALE = 96.0 ** -0.5


def build_ME(NKT, NCT, HH=4):
    nc = bass.Bass("TRN2", target_bir_lowering=False)
    dr = lambda n, s, k="ExternalInput": nc.dram_tensor(n, list(s), F32, kind=k).ap()
    NK = NKT * 128
    NC = NCT * 128
    QT = dr("QT", [HH, 128, NK])
    KT = dr("KT", [HH, 128, NK])
    VA = dr("VA", [HH, 128, NKT, 65])
    KN2 = dr("KN2", [HH, 1, NK])
    SEL = dr("SEL", [65, 64])
    RQT = dr("RQT", [HH, 64, NK])
    RKT = dr("RKT", [HH, 64, NK])
    RK = dr("RK", [HH, 128, NKT, 64])
    RV = dr("RV", [HH, 128, NKT, 64])
    DSYM = dr("DSYM", [HH, 128, 128])
    DCOL = dr("DCOL", [128, HH, 4])
    GC = dr("GC", [64, HH])
    OT = dr("OT", [HH, 64, NK], "ExternalOutput")
    RO = dr("RO", [HH, 128, NKT, 64], "ExternalOutput")
    obuf = Buf("outs")

    with ExitStack() as st:
        kb = KB(nc, st)
        G1, G1b = kb.sb([128, NK])
        G2, G2b = kb.sb([128, NK])
        G3, G3b = kb.sb([128, NKT * 65])
        G4, G4b = kb.sb([128, NKT * 65])
        G5, G5b = kb.sb([128, NKT * 64])
        SF, SFb = kb.sb([64, NKT + 1, 64])
        SB_, SBb = kb.sb([64, NKT + 1, 64])
        sel, selb = kb.sb([65, 64])
        kb.dma("sp", sel[:], SEL, writes=[selb])
        dcol, dcolb = kb.sb([128, HH, 4])
        kb.dma("sp", dcol[:], DCOL, writes=[dcolb])
        gc, gcb = kb.sb([64, HH])
        kb.dma("sp", gc[:], GC, writes=[gcb])
        kn2, kn2b = kb.sb([1, NK])
        km, kmb = kb.sb([1, 1])
        qblk = [kb.sb([128, 512]) for _ in range(2)]
        pts = [kb.sb([128, 512]) for _ in range(3)]
        oa, oab = kb.sb([65, 512])
        rb, rbb = kb.sb([64, 512])
        ob_, obb = kb.sb([64, 512])
        pS = [kb.ps([128, 512]) for _ in range(2)]
        pO, pOb = kb.ps([65, 512])
        pB, pBb = kb.ps([64, 512])
        qblocks = [(0, NC, 0, NCT)] if NCT > 0 else []
        for c0 in range(NC, NK, 512):
            qblocks.append((c0, min(512, NK - c0), 0, NKT))
        it = 0
        for h in range(HH):
            kb.dma("sp", G1[:], KT[h], writes=[G1b])
            kb.dma("sp", G4[:].rearrange("p (n d) -> p n d", d=65), VA[h], writes=[G4b])
            kb.dma("sp", kn2[:], KN2[h], writes=[kn2b])
            kb.red(km[:], kn2[:], ALU.max, reads=[kn2b], writes=[kmb])
            va = G4[:].rearrange("p (n d) -> p n d", d=65)
            for (c0, cw, k0, k1) in qblocks:
                q, qb_ = qblk[it % 2]
                it += 1
                kb.dma("sp", q[:, 0:cw], QT[h][:, c0:c0 + cw], writes=[qb_])
                kb.ts(q[0:1, 0:cw], q[0:1, 0:cw], km[0:1, 0:1], None, ALU.mult, reads=[qb_, kmb], writes=[qb_])
                kb.act(q[0:1, 0:cw], q[0:1, 0:cw], AF.Sqrt, reads=[qb_], writes=[qb_])
                kb.ts(q[0:1, 0:cw], q[0:1, 0:cw], -1.0, None, ALU.mult, reads=[qb_], writes=[qb_])
                nkt = k1 - k0

                def smm(i):
                    ps, psb = pS[i % 2]
                    kt = k0 + i
                    kb.mm(ps[:, 0:cw], G1[:, kt * 128:(kt + 1) * 128], q[:, 0:cw], True, True,
                          reads=[G1b, qb_], writes=[psb])
                smm(0)
                for i in range(nkt):
                    ps, psb = pS[i % 2]
                    pt, ptb = pts[i % 3]
                    kb.act(pt[:, 0:cw], ps[:, 0:cw], AF.Exp, reads=[psb], writes=[ptb], scale=MLA_SCALE)
                    if i + 1 < nkt:
                        smm(i + 1)
                    kb.mm(pO[:, 0:cw], va[:, k0 + i, :], pt[:, 0:cw], i == 0, i == nkt - 1,
                          reads=[G4b, ptb], writes=[pOb])
                kb.copy(oa[:, 0:cw], pO[:, 0:cw], reads=[pOb], writes=[oab])
                kb.mm(pB[:, 0:cw], sel[:], oa[:, 0:cw], True, True, reads=[selb, oab], writes=[pBb])
                kb.op("dve", lambda: nc.vector.reciprocal(out=rb[:, 0:cw], in_=pB[:, 0:cw]),
                      reads=[pBb], writes=[rbb])
                kb.tt(ob_[:, 0:cw], oa[0:64, 0:cw], rb[:, 0:cw], ALU.mult, reads=[oab, rbb], writes=[obb])
                kb.dma("sp", OT[h][:, c0:c0 + cw], ob_[:, 0:cw], reads=[obb], writes=[obuf], nowaw=True)
        dsym, dsymb = kb.sb([128, 128])
        at = [kb.sb([128, 128]) for _ in range(2)]
        osb = [kb.sb([128, 64]) for _ in range(2)]
        pKV, pKVb = kb.ps([64, 512])
        pR = pS
        rk = G3[:, 0:NKT * 64].rearrange("p (n d) -> p n d", d=64)
        rv = G4[:, 0:NKT * 64].rearrange("p (n d) -> p n d", d=64)
        kd = G5[:].rearrange("p (n d) -> p n d", d=64)
        fwd_order = list(range(NKT))
        bwd_order = list(range(NCT - 1, -1, -1)) + list(range(NKT - 1, NCT - 1, -1))
        for h in range(HH):
            kb.dma("sp", G1[0:64, :], RQT[h], writes=[G1b])
            kb.dma("sp", G2[0:64, :], RKT[h], writes=[G2b])
            kb.dma("sp", rk, RK[h], writes=[G3b])
            kb.dma("sp", rv, RV[h], writes=[G4b])
            kb.dma("sp", dsym[:], DSYM[h], writes=[dsymb])
            for (di, order, S_, S_b) in ((0, fwd_order, SF, SFb), (1, bwd_order, SB_, SBb)):
                kb.ts(G5[:], G3[:, 0:NKT * 64], dcol[:, h, di:di + 1], None, ALU.mult,
                      reads=[G3b, dcolb], writes=[G5b])
                kb.memset(S_[:, order[0], :], 0.0, writes=[S_b])
                for g0 in range(0, NKT, 8):
                    grp = order[g0:g0 + 8]
                    for j, c in enumerate(grp):
                        kb.mm(pKV[:, j * 64:(j + 1) * 64], kd[:, c, :], rv[:, c, :], True, True,
                              reads=[G5b, G4b], writes=[pKVb])
                    for j, c in enumerate(grp):
                        idx = g0 + j
                        nxt = order[idx + 1] if idx + 1 < NKT else NKT
                        kb.stt(S_[:, nxt, :], S_[:, c, :], gc[:, h:h + 1], pKV[:, j * 64:(j + 1) * 64],
                               ALU.mult, ALU.add, reads=[S_b, gcb, pKVb], writes=[S_b])
            for n in range(NKT):
                pr, prb = pR[n % 2]
                a_, a_b = at[n % 2]
                o_, o_b = osb[n % 2]
                sl = slice(n * 128, (n + 1) * 128)
                kb.mm(pr[:, 0:128], G2[0:64, sl], G1[0:64, sl], True, True, reads=[G1b, G2b], writes=[prb])
                kb.tt(a_[:], pr[:, 0:128], dsym[:], ALU.mult, reads=[prb, dsymb], writes=[a_b])
                kb.mm(pr[:, 128:192], a_[:], rv[:, n, :], True, True, reads=[a_b, G4b], writes=[prb])
                kb.mm(pr[:, 192:256], G1[0:64, sl], SF[:, n, :], True, True, reads=[G1b, SFb], writes=[prb])
                kb.mm(pr[:, 256:320], G1[0:64, sl], SB_[:, n, :], True, True, reads=[G1b, SBb], writes=[prb])
                kb.copy(o_[:], pr[:, 128:192], reads=[prb], writes=[o_b], e="act")
                kb.stt(o_[:], pr[:, 192:256], dcol[:, h, 2:3], o_[:], ALU.mult, ALU.add,
                       reads=[prb, dcolb, o_b], writes=[o_b])
                kb.stt(o_[:], pr[:, 256:320], dcol[:, h, 3:4], o_[:], ALU.mult, ALU.add,
                       reads=[prb, dcolb, o_b], writes=[o_b])
                kb.dma("sp", RO[h][:, n, :], o_[:], reads=[o_b], writes=[obuf], nowaw=True)
        kb.finish([obuf])
    return nc


class Pool:
    def __init__(self, items):
        self.items = items
        self.i = 0

    def get(self):
        it = self.items[self.i % len(self.items)]
        self.i += 1
        return it


def roundrobin(gens):
    gens = list(gens)
    while gens:
        nxt = []
        for g in gens:
            try:
                next(g)
                nxt.append(g)
            except StopIteration:
                pass
        gens = nxt


def build_MO(NKT, NCT, HH=4):
    nc = bass.Bass("TRN2", target_bir_lowering=False)
    dr = lambda n, s, k="ExternalInput": nc.dram_tensor(n, list(s), F32, kind=k).ap()
    NK = NKT * 128
    NC = NCT * 128
    PZ = dr("PZ", [3, HH, 128, NK + 8])
    CW = dr("CW", [HH, 128, 3, 5])
    GG = dr("GG", [128, 2, NKT, HH])
    BB = dr("BB", [128, 2, NKT, HH])
    CM = dr("CM", [128, 10, 128])
    OUT = dr("OUT", [2, HH, 128, NKT, 128], "ExternalOutput")
    obuf = Buf("outs")
    NS = NKT * HH

    with ExitStack() as st:
        kb = KB(nc, st)
        cm, cmb = kb.sb([128, 10, 128])
        kb.dma("sp", cm[:], CM, writes=[cmb])
        TRI = lambda d: cm[:, 4 * d + 0, :]
        UU = lambda d: cm[:, 4 * d + 1, :]
        MINC = lambda d: cm[:, 4 * d + 2, :]
        MSTR = lambda d: cm[:, 4 * d + 3, :]
        ident = cm[:, 8, :]
        ones = cm[:, 9, :]
        QT, QTb = kb.sb([128, NK])
        KT, KTb = kb.sb([128, NK])
        VTOK, VTOKb = kb.sb([128, NKT, 128])
        import os
        NSP = NS
        gg, ggb = kb.sb([128, 2 * NSP])
        bt, btb = kb.sb([128, 2 * NSP])
        ee, eeb = kb.sb([128, 2 * NSP])
        ne, neb = kb.sb([128, 2 * NSP])
        dd, ddb = kb.sb([128, 2 * NSP])
        cd, cdb = kb.sb([128, 2 * NSP])
        if os.environ.get("V1") != "1":
            kb.memset(gg[:], 0.0, writes=[ggb])
            kb.memset(bt[:], 0.0, writes=[btb])
        for d in range(2):
            kb.dma("sp", gg[:, d * NSP:d * NSP + NS], GG[:, d].rearrange("p n h -> p (n h)"), writes=[ggb])
            kb.dma("sp", bt[:, d * NSP:d * NSP + NS], BB[:, d].rearrange("p n h -> p (n h)"), writes=[btb])
        big = [kb.ps([128, 512]) for _ in range(2)]
        for d in range(2):
            p1, p1b = big[0]
            p2, p2b = big[1]
            sl = slice(d * NSP, (d + 1) * NSP)
            for c0 in range(0, NSP, 512):
                cwd = min(512, NSP - c0)
                s2 = slice(d * NSP + c0, d * NSP + c0 + cwd)
                PRO = int(os.environ.get("PRO", "9"))
                if PRO >= 2:
                    kb.mm(p1[:, 0:cwd], TRI(d), gg[:, s2], True, True, reads=[cmb, ggb], writes=[p1b])
                    kb.mm(p2[:, 0:cwd], ones, gg[:, s2], True, True, reads=[cmb, ggb], writes=[p2b])
                if PRO >= 3:
                    kb.act(ee[:, s2], p1[:, 0:cwd], AF.Exp, reads=[p1b], writes=[eeb])
                    kb.act(cd[:, s2], p2[:, 0:cwd], AF.Exp, reads=[p2b], writes=[cdb])
                if PRO >= 4:
                    kb.copy(dd[:, s2], p2[:, 0:cwd], reads=[p2b], writes=[ddb], e="act")
                    kb.copy(ne[:, s2], p1[:, 0:cwd], reads=[p1b], writes=[neb], e="act")
                    kb.tt(dd[:, s2], dd[:, s2], ne[:, s2], ALU.subtract, reads=[ddb, neb], writes=[ddb])
                if PRO >= 5:
                    kb.act(dd[:, s2], dd[:, s2], AF.Exp, reads=[ddb], writes=[ddb])
        if PRO >= 6:
            kb.ts(ne[:], ee[:], -1.0, None, ALU.mult, reads=[eeb], writes=[neb])
        col = lambda t, d, n, h: t[:, d * NSP + n * HH + h: d * NSP + n * HH + h + 1]

        cw, cwb = kb.sb([128, 3, 5])
        pin = [kb.sb([128, 516]) for _ in range(2)]
        yb = [kb.sb([128, 512]) for _ in range(2)]
        y2, y2b = kb.sb([128, 512])
        rs, rsb = kb.sb([128, 512])
        qtiles = [kb.ps([128, 512]) for _ in range(5)]
        pslots = Pool([(t[:, j * 128:(j + 1) * 128], b) for j in range(4) for (t, b) in qtiles])
        mt = [kb.sb([128, 128]) for _ in range(44)]
        mpool = Pool([(t[:], b) for (t, b) in mt])
        S = [kb.sb([128, 128]) for _ in range(2)]
        fin = []
        for d in range(2):
            fin.append([])
            for j in range(2):
                a1, b1 = kb.sb([128, 128]); a2, b2 = kb.sb([128, 128]); a3, b3 = kb.sb([128, 128])
                fin[d].append((a1[:], b1, a2[:], b2, a3[:], b3))
        orders = [list(range(NKT)),
                  list(range(NCT - 1, -1, -1)) + list(range(NKT - 1, NCT - 1, -1))]

        blocks = ([(0, NC, 0)] if NCT else []) + [(c0, min(512, NK - c0), 4) for c0 in range(NC, NK, 512)]
        import os
        STG = int(os.environ.get("MO_STAGE", "9"))
        SUB = int(os.environ.get("MO_SUB", "99"))
        for h in range(HH if STG >= 1 else 0):
            kb.dma("sp", cw[:], CW[h], writes=[cwb])
            bi = 0
            for (c0, cwid, off) in blocks:
                for w in range(3):
                    pi, pib = pin[bi % 2]
                    y, ybb = yb[bi % 2]
                    bi += 1
                    kb.dma("sp", pi[:, 0:cwid + 4], PZ[w, h][:, c0 + off:c0 + off + cwid + 4], writes=[pib])
                    kb.ts(y[:, 0:cwid], pi[:, 0:cwid], cw[:, w, 0:1], None, ALU.mult, reads=[pib, cwb], writes=[ybb])
                    for j in range(1, 5):
                        kb.stt(y[:, 0:cwid], pi[:, j:j + cwid], cw[:, w, j:j + 1], y[:, 0:cwid], ALU.mult, ALU.add,
                               reads=[pib, cwb, ybb], writes=[ybb])
                    kb.act(y[:, 0:cwid], y[:, 0:cwid], AF.Silu, reads=[ybb], writes=[ybb])
                    if w < 2:
                        pb_, pbb = big[0]
                        kb.act(y2[:, 0:cwid], y[:, 0:cwid], AF.Square, reads=[ybb], writes=[y2b])
                        kb.mm(pb_[:, 0:cwid], ones, y2[:, 0:cwid], True, True, reads=[cmb, y2b], writes=[pbb])
                        kb.ts(rs[:, 0:cwid], pb_[:, 0:cwid], 1e-6, None, ALU.add, reads=[pbb], writes=[rsb])
                        kb.act(rs[:, 0:cwid], rs[:, 0:cwid], AF.Sqrt, reads=[rsb], writes=[rsb])
                        kb.op("dve", lambda: nc.vector.reciprocal(out=rs[:, 0:cwid], in_=rs[:, 0:cwid]),
                              reads=[rsb], writes=[rsb])
                        dst, dstb = (QT, QTb) if w == 0 else (KT, KTb)
                        if w == 0:
                            kb.stt(dst[:, c0:c0 + cwid], y[:, 0:cwid], 128.0 ** -0.5, rs[:, 0:cwid], ALU.mult, ALU.mult,
                                   reads=[ybb, rsb], writes=[dstb])
                        else:
                            kb.tt(dst[:, c0:c0 + cwid], y[:, 0:cwid], rs[:, 0:cwid], ALU.mult,
                                  reads=[ybb, rsb], writes=[dstb])
                    if w == 2:
                        src, srcb = (y[:, 0:cwid], ybb)
                        dst, dstb = (VTOK, VTOKb)
                        pb_, pbb = big[1]
                        for j in range(cwid // 128):
                            kb.tr(pb_[:, j * 128:(j + 1) * 128], src[:, j * 128:(j + 1) * 128], ident,
                                  reads=[srcb, cmb], writes=[pbb])
                        kb.copy(dst[:, c0 // 128:(c0 + cwid) // 128, :],
                                pb_[:, 0:cwid].rearrange("p (n d) -> p n d", d=128), reads=[pbb], writes=[dstb], e="act")

            if STG < 2:
                continue
            pre_done = [0, 0]
            scan_pos = [0, 0]

            def precompute(i, d):
                n = orders[d][i]
                while scan_pos[d] < i - 1 and STG >= 3:
                    yield
                fMT, fMTb, fAT, fATb, fKD, fKDb = fin[d][i % 2]
                sl = slice(n * 128, (n + 1) * 128)
                pkk, pkkb = pslots.get()
                pqk, pqkb = pslots.get()
                kb.mm(pkk, KT[:, sl], KT[:, sl], True, True, reads=[KTb], writes=[pkkb])
                kb.mm(pqk, KT[:, sl], QT[:, sl], True, True, reads=[KTb, QTb], writes=[pqkb])
                B, Bb = mpool.get()
                kb.ts(B, UU(d), col(gg, d, n, h), None, ALU.mult, reads=[cmb, ggb], writes=[Bb])
                pg, pgb = pslots.get()
                kb.mm(pg, B, TRI(d), True, True, reads=[Bb, cmb], writes=[pgb])
                yield
                if SUB <= 1:
                    pre_done[d] = i + 1
                    return
                E, Eb = mpool.get()
                kb.act(E, pg, AF.Exp, reads=[pgb], writes=[Eb])
                yield
                if SUB <= 2:
                    pre_done[d] = i + 1
                    return
                Es, Esb = mpool.get()
                Ei, Eib = mpool.get()
                kb.tt(Es, E, MSTR(d), ALU.mult, reads=[Eb, cmb], writes=[Esb])
                kb.tt(Ei, E, MINC(d), ALU.mult, reads=[Eb, cmb], writes=[Eib])
                yield
                if SUB <= 3:
                    pre_done[d] = i + 1
                    return
                X, Xb = mpool.get()
                kb.stt(X, pkk, col(bt, d, n, h), Es, ALU.mult, ALU.mult, reads=[pkkb, btb, Esb], writes=[Xb])
                kb.tt(fAT, pqk, Ei, ALU.mult, reads=[pqkb, Eib], writes=[fATb])
                yield
                if SUB <= 4:
                    pre_done[d] = i + 1
                    return
                pk, pkb = pslots.get()
                kb.tr(pk, KT[:, sl], ident, reads=[KTb, cmb], writes=[pkb])
                kb.ts(fKD, pk, col(dd, d, n, h), None, ALU.mult, reads=[pkb, ddb], writes=[fKDb])
                px, pxb = pslots.get()
                kb.tr(px, X, ident, reads=[Xb, cmb], writes=[pxb])
                XT, XTb = mpool.get()
                kb.copy(XT, px, reads=[pxb], writes=[XTb], e="act")
                P, Pb = mpool.get()
                kb.tt(P, ident, X, ALU.subtract, reads=[cmb, Xb], writes=[Pb])
                yield
                if SUB <= 5:
                    pre_done[d] = i + 1
                    return
                A, Ab, ATr, ATrb = X, Xb, XT, XTb
                for j in range(6):
                    last = j == 5
                    p2t, p2tb = pslots.get()
                    kb.mm(p2t, A, ATr, True, True, reads=[Ab, ATrb], writes=[p2tb])
                    if not last:
                        p2, p2b_ = pslots.get()
                        kb.mm(p2, ATr, A, True, True, reads=[Ab, ATrb], writes=[p2b_])
                    yield
                    if SUB <= 6:
                        pre_done[d] = i + 1
                        return
                    nT, nTb = mpool.get()
                    kb.copy(nT, p2t, reads=[p2tb], writes=[nTb], e="act")
                    if not last:
                        nA, nAb = mpool.get()
                        kb.copy(nA, p2, reads=[p2b_], writes=[nAb])
                    yield
                    if SUB <= 7:
                        pre_done[d] = i + 1
                        return
                    pp, ppb = pslots.get()
                    kb.mm(pp, nT, P, True, True, reads=[nTb, Pb], writes=[ppb])
                    yield
                    if SUB <= 8:
                        pre_done[d] = i + 1
                        return
                    nP, nPb = (fMT, fMTb) if last else mpool.get()
                    kb.tt(nP, pp, P, ALU.add, reads=[ppb, Pb], writes=[nPb])
                    P, Pb = nP, nPb
                    if not last:
                        A, Ab, ATr, ATrb = nA, nAb, nT, nTb
                    yield
                    if SUB <= 9:
                        pre_done[d] = i + 1
                        return
                pre_done[d] = i + 1

            def scan(d):
                S_, Sb_ = S[d]
                kb.memset(S_[:], 0.0, writes=[Sb_], e="pool")
                for i, n in enumerate(orders[d]):
                    while pre_done[d] < i + 1:
                        yield
                    MT, MTb, AT, ATb, KD, KDb = fin[d][i % 2]
                    sl = slice(n * 128, (n + 1) * 128)
                    pks, pksb = pslots.get()
                    pqs, pqsb = pslots.get()
                    kb.mm(pks, KT[:, sl], S_[:], True, True, reads=[KTb, Sb_], writes=[pksb])
                    kb.mm(pqs, QT[:, sl], S_[:], True, True, reads=[QTb, Sb_], writes=[pqsb])
                    yield
                    R, Rb = mpool.get()
                    kb.stt(R, pks, col(ne, d, n, h), VTOK[:, n, :], ALU.mult, ALU.add,
                           reads=[pksb, neb, VTOKb], writes=[Rb])
                    yield
                    pmr, pmrb = pslots.get()
                    kb.mm(pmr, MT, R, True, True, reads=[MTb, Rb], writes=[pmrb])
                    yield
                    VN, VNb = mpool.get()
                    kb.ts(VN, pmr, col(bt, d, n, h), None, ALU.mult, reads=[pmrb, btb], writes=[VNb])
                    yield
                    pav, pavb = pslots.get()
                    psn, psnb = pslots.get()
                    kb.mm(psn, KD, VN, True, True, reads=[KDb, VNb], writes=[psnb])
                    kb.mm(pav, AT, VN, True, True, reads=[ATb, VNb], writes=[pavb])
                    yield
                    kb.stt(S_[:], S_[:], col(cd, d, n, h), psn, ALU.mult, ALU.add,
                           reads=[Sb_, cdb, psnb], writes=[Sb_])
                    O1, O1b = mpool.get()
                    kb.copy(O1, pav, reads=[pavb], writes=[O1b], e="act")
                    yield
                    kb.stt(O1, pqs, col(ee, d, n, h), O1, ALU.mult, ALU.add, reads=[pqsb, eeb, O1b], writes=[O1b])
                    kb.dma("sp", OUT[d, h][:, n, :], O1, reads=[O1b], writes=[obuf], nowaw=True)
                    scan_pos[d] = i + 1
                    yield

            def pre_all(d):
                for i in range(NKT):
                    yield from precompute(i, d)

            if STG < 3:
                roundrobin([pre_all(0), pre_all(1)])
            else:
                roundrobin([pre_all(0), pre_all(1), scan(0), scan(1)])
        kb.finish([obuf])
    return nc


def mo_consts():
    j = np.arange(128)[:, None]
    c = np.arange(128)[None, :]
    m = np.zeros((128, 10, 128), np.float32)
    m[:, 0] = (j <= c)
    m[:, 1] = (j > c)
    m[:, 2] = (j <= c)
    m[:, 3] = (j < c)
    m[:, 4] = (j >= c)
    m[:, 5] = (j < c)
    m[:, 6] = (j >= c)
    m[:, 7] = (j > c)
    m[:, 8] = np.eye(128)
    m[:, 9] = 1.0
    return m


def emit_ln(kb, dst, dstb, src, srcb, g, gb, b, bb, st6, st6b, mv, mvb):
    nc = kb.nc
    for j in range(2):
        kb.op("dve", lambda: nc.vector.bn_stats(out=st6[:, j, :], in_=src[:, j * 512:(j + 1) * 512]),
              reads=[srcb], writes=[st6b])
    kb.op("dve", lambda: nc.vector.bn_aggr(out=mv[:, 0:2], in_=st6[:].rearrange("p a b -> p (a b)")),
          reads=[st6b], writes=[mvb])
    emit_rstd(kb, mv[:, 1:2], mv[:, 1:2], 1.0, 1e-5, reads=[mvb], writes=[mvb])
    kb.ts(dst[:], src[:], mv[:, 0:1], mv[:, 1:2], ALU.subtract, ALU.mult, reads=[srcb, mvb], writes=[dstb])
    kb.tt(dst[:], dst[:], g[:], ALU.mult, reads=[dstb, gb], writes=[dstb])
    kb.tt(dst[:], dst[:], b[:], ALU.add, reads=[dstb, bb], writes=[dstb])


def build_C(even, NT, NL):
    nc = bass.Bass("TRN2", target_bir_lowering=False)
    dr = lambda n, s, k="ExternalInput", dt=F32: nc.dram_tensor(n, list(s), dt, kind=k).ap()
    x = dr("x", [NT, 128, D])
    if even:
        mixd = dr("mix", [NT, 128, 1024])
        gated = dr("gate", [NT, 128, 512])
        gng = dr("gn_g", [512])
    else:
        mixd = dr("mix", [2, NT, 128, 1024])
        gated = dr("gate", [NT, 128, 1024])
        gng = dr("gn_g", [128])
    w_out = dr("w_out", [D, D])
    cvl = dr("cvl", [128, 8])
    cvc = dr("cvc", [128, 8])
    ada_w = dr("ada_w", [D, 4096])
    ada_b = dr("ada_b", [4096])
    lnp = dr("lnp", [4, D])
    w_q = dr("w_q", [D, 2048])
    kTd = dr("kT", [128, 16, 128])
    u_tab = dr("u_tab", [16384, D])
    v_tab = dr("v_tab", [16384, D])
    cst = dr("cst", [128, 144])
    out = dr("out", [NT, 128, D], "ExternalOutput")
    obuf = Buf("outs")

    with ExitStack() as st:
        kb = KB(nc, st)
        cs, csb = kb.sb([128, 144])
        kb.dma("sp", cs[:], cst, writes=[csb])
        ident = cs[:, 0:128]
        iota = cs[:, 128:144]
        mod, modb = kb.sb([128, 4096])
        ln, lnb = kb.sb([128, 4, D])
        for j in range(4):
            kb.bload("sp", ln[:, j, :], lnp[j], D, writes=[lnb])
        gg, ggb = kb.sb([128, 512 if even else 128])
        kb.bload("sp", gg[:], gng, 512 if even else 128, writes=[ggb])
        kT, kTb = kb.sb([128, 16, 128])
        kb.dma("sp", kT[:], kTd, writes=[kTb])
        wch = [kb.sb([128, 8, 512]) for _ in range(2)]
        xt, xtb = kb.sb([128, D])
        mix, mixb = kb.sb([128, D])
        mix2, mix2b = kb.sb([128, D])
        gt, gtb = kb.sb([128, D])
        cT, cTb = kb.sb([128, 8, 128])
        tt_, ttb = kb.sb([128, D])
        x1, x1b = kb.sb([128, D])
        xb_, xbb = kb.sb([128, D])
        qT, qTb = kb.sb([128, 16, 128])
        ssb, ssbb = kb.sb([128, 16, 128])
        acc, accb = kb.sb([128, D])
        ub = [kb.sb([128, D]) for _ in range(3)]
        vb = [kb.sb([128, D]) for _ in range(3)]
        junk, junkb = kb.sb([128, D])
        st6, st6b = kb.sb([128, 2, 6])
        mv, mvb = kb.sb([128, 2])
        sm, smb = kb.sb([128, 16])
        vals, valsb = kb.sb([128, 16, 16])
        idxu, idxub = kb.sb([128, 16, 16], U32)
        idxf, idxfb = kb.sb([128, 16, 16])
        tmp, tmpb = kb.sb([128, 256])
        cand, candb = kb.sb([128, 256])
        sc16, sc16b = kb.sb([128, 16])
        jpu, jpub = kb.sb([128, 16], U32)
        jau, jaub = kb.sb([128, 2, 16], U32)
        jaf, jafb = kb.sb([128, 2, 16])
        oh, ohb = kb.sb([128, 16, 16])
        isel, iselb = kb.sb([128, 2, 16])
        ef, efb = kb.sb([128, 16])
        ei = [kb.sb([128, 16], I32) for _ in range(8)]
        nmx, nmxb = kb.sb([128, 8])
        exa, exab = kb.sb([128, 8, 16])
        rsa, rsab = kb.sb([128, 8])
        ex, exb = kb.sb([128, 16])
        dots, dotsb = kb.sb([128, 16])
        wg = [kb.sb([128, 16]) for _ in range(2)]
        pbig = [kb.ps([128, 512]) for _ in range(4)]
        pz = [kb.ps([128, 512]) for _ in range(2)]
        NEG = -1.0e30
        wi = 0

        def top16(src, srcb, n, vout, voutb, iout, ioutb):
            kb.op("dve", lambda: nc.vector.max(out=vout[:, 0:8], in_=src), reads=[srcb], writes=[voutb])
            kb.op("dve", lambda: nc.vector.match_replace(out=tmp[:, 0:n], in_to_replace=vout[:, 0:8],
                                                         in_values=src, imm_value=NEG),
                  reads=[srcb, voutb], writes=[tmpb])
            kb.op("dve", lambda: nc.vector.max(out=vout[:, 8:16], in_=tmp[:, 0:n]), reads=[tmpb], writes=[voutb])
            kb.op("dve", lambda: nc.vector.max_index(out=iout[:, 0:8], in_max=vout[:, 0:8], in_values=src),
                  reads=[srcb, voutb], writes=[ioutb])
            kb.op("dve", lambda: nc.vector.max_index(out=iout[:, 8:16], in_max=vout[:, 8:16], in_values=src),
                  reads=[srcb, voutb], writes=[ioutb])

        for t in range(NT):
            if t == 0:
                emit_mod(kb, cvl, ada_w, ada_b, 0, 4096, mod, modb)
                kb.ts(mod[:, 2048:3072], mod[:, 2048:3072], 1.0, None, ALU.add, reads=[modb], writes=[modb])
            elif t == NL:
                emit_mod(kb, cvc, ada_w, ada_b, 0, 4096, mod, modb)
                kb.ts(mod[:, 2048:3072], mod[:, 2048:3072], 1.0, None, ALU.add, reads=[modb], writes=[modb])
            kb.dma("sp", xt[:], x[t], writes=[xtb])
            if even:
                kb.dma("sp", mix[:], mixd[t], writes=[mixb])
                kb.dma("sp", gt[:, 0:512], gated[t], writes=[gtb])
                rv = mix[:, 512:1024].rearrange("p (h d) -> p h d", h=8)
                kb.red(sm[:, 0:8], rv, ALU.add, reads=[mixb], writes=[smb])
                kb.ts(sm[:, 0:8], sm[:, 0:8], 1.0 / 64, None, ALU.mult, reads=[smb], writes=[smb])
                kb.tt(rv, rv, sm[:, 0:8].unsqueeze(2).to_broadcast([128, 8, 64]), ALU.subtract,
                      reads=[mixb, smb], writes=[mixb])
                jv = junk[:, 0:512].rearrange("p (h d) -> p h d", h=8)
                kb.tt(jv, rv, rv, ALU.mult, reads=[mixb], writes=[junkb])
                kb.red(sm[:, 8:16], jv, ALU.add, reads=[junkb], writes=[smb])
                emit_rstd(kb, sm[:, 8:16], sm[:, 8:16], 1.0 / 64, 1e-5, reads=[smb], writes=[smb])
                kb.tt(rv, rv, sm[:, 8:16].unsqueeze(2).to_broadcast([128, 8, 64]), ALU.mult,
                      reads=[mixb, smb], writes=[mixb])
                kb.tt(mix[:, 512:1024], mix[:, 512:1024], gg[:], ALU.mult, reads=[mixb, ggb], writes=[mixb])
                kb.act(gt[:, 0:512], gt[:, 0:512], AF.Silu, reads=[gtb], writes=[gtb])
                kb.tt(mix[:, 512:1024], mix[:, 512:1024], gt[:, 0:512], ALU.mult, reads=[mixb, gtb], writes=[mixb])
            else:
                kb.dma("sp", mix[:], mixd[0, t], writes=[mixb])
                kb.dma("sp", mix2[:], mixd[1, t], writes=[mix2b])
                kb.dma("sp", gt[:], gated[t], writes=[gtb])
                kb.tt(mix[:], mix[:], mix2[:], ALU.add, reads=[mixb, mix2b], writes=[mixb])
                ov = mix[:].rearrange("p (h d) -> p h d", h=8)
                jv = junk[:].rearrange("p (h d) -> p h d", h=8)
                kb.tt(jv, ov, ov, ALU.mult, reads=[mixb], writes=[junkb])
                kb.red(sm[:, 0:8], jv, ALU.add, reads=[junkb], writes=[smb])
                emit_rstd(kb, sm[:, 0:8], sm[:, 0:8], 1.0 / 128, 1e-6, reads=[smb], writes=[smb])
                kb.tt(ov, ov, sm[:, 0:8].unsqueeze(2).to_broadcast([128, 8, 128]), ALU.mult,
                      reads=[mixb, smb], writes=[mixb])
                kb.tt(ov, ov, gg[:].unsqueeze(1).to_broadcast([128, 8, 128]), ALU.mult,
                      reads=[mixb, ggb], writes=[mixb])
                kb.act(gt[:], gt[:], AF.Silu, reads=[gtb], writes=[gtb])
                kb.tt(mix[:], mix[:], gt[:], ALU.mult, reads=[mixb, gtb], writes=[mixb])

            def transpose8(src, srcb):
                for half in range(2):
                    p, pb = pbig[half]
                    for k in range(4):
                        kk = half * 4 + k
                        kb.tr(p[:, k * 128:(k + 1) * 128], src[:, kk * 128:(kk + 1) * 128], ident,
                              reads=[srcb, csb], writes=[pb])
                    kb.copy(cT[:, half * 4:half * 4 + 4, :], p[:].rearrange("p (k n) -> p k n", k=4),
                            reads=[pb], writes=[cTb], e="act")
            transpose8(mix, mixb)
            for c in range(2):
                w, wb = wch[wi % 2]
                pp, ppb = pz[wi % 2]
                wi += 1
                kb.dma("sp", w[:], w_out[:, c * 512:(c + 1) * 512].rearrange("(k p) n -> p k n", p=128), writes=[wb])
                for k in range(8):
                    kb.mm(pp[:], cT[:, k, :], w[:, k, :], k == 0, k == 7, reads=[cTb, wb], writes=[ppb])
                kb.tt(tt_[:, c * 512:(c + 1) * 512], pp[:], mod[:, c * 512:(c + 1) * 512], ALU.mult,
                      reads=[ppb, modb], writes=[ttb])
            kb.stt(tt_[:], xt[:], ALPHA, tt_[:], ALU.mult, ALU.add, reads=[xtb, ttb], writes=[ttb])
            emit_ln(kb, x1, x1b, tt_, ttb, ln[:, 0, :], lnb, ln[:, 1, :], lnb, st6, st6b, mv, mvb)
            kb.tt(xb_[:], x1[:], mod[:, 2048:3072], ALU.mult, reads=[x1b, modb], writes=[xbb])
            kb.tt(xb_[:], xb_[:], mod[:, 1024:2048], ALU.add, reads=[xbb, modb], writes=[xbb])
            transpose8(xb_, xbb)
            for c in range(4):
                w, wb = wch[wi % 2]
                pp, ppb = pz[wi % 2]
                wi += 1
                kb.dma("sp", w[:], w_q[:, c * 512:(c + 1) * 512].rearrange("(k p) n -> p k n", p=128), writes=[wb])
                for j in range(4):
                    for k in range(8):
                        kb.mm(pp[:, j * 128:(j + 1) * 128], w[:, k, j * 128:(j + 1) * 128], cT[:, k, :],
                              k == 0, k == 7, reads=[cTb, wb], writes=[ppb])
                kb.copy(qT[:, c * 4:(c + 1) * 4, :], pp[:].rearrange("p (j n) -> p j n", j=4),
                        reads=[ppb], writes=[qTb], e="act")
            for c in range(4):
                pp, ppb = pbig[c]
                for j in range(4):
                    jj = c * 4 + j
                    kb.mm(pp[:, j * 128:(j + 1) * 128], qT[:, jj, :], kT[:, jj, :], True, True,
                          reads=[qTb, kTb], writes=[ppb])
                kb.copy(ssb[:, c * 4:(c + 1) * 4, :], pp[:].rearrange("p (j n) -> p j n", j=4),
                        reads=[ppb], writes=[ssbb], e=("act" if c % 2 else "dve"))
            for j in range(16):
                top16(ssb[:, j, :], ssbb, 128, vals[:, j, :], valsb, idxu[:, j, :], idxub)
            kb.copy(idxf[:].rearrange("p a b -> p (a b)"), idxu[:].rearrange("p a b -> p (a b)"),
                    reads=[idxub], writes=[idxfb])
            kb.memset(acc[:], 0.0, writes=[accb], e="pool")
            for h in range(8):
                v1 = vals[:, 2 * h, :]
                v2 = vals[:, 2 * h + 1, :]
                cv = cand[:].rearrange("p (a b) -> p a b", a=16)
                kb.tt(cv, v1.unsqueeze(2).to_broadcast([128, 16, 16]), v2.unsqueeze(1).to_broadcast([128, 16, 16]),
                      ALU.add, reads=[valsb], writes=[candb])
                top16(cand[:], candb, 256, sc16, sc16b, jpu, jpub)
                kb.ts(jau[:, 0, :], jpu[:], 4, None, ALU.logical_shift_right, reads=[jpub], writes=[jaub])
                kb.ts(jau[:, 1, :], jpu[:], 15, None, ALU.bitwise_and, reads=[jpub], writes=[jaub])
                kb.copy(jaf[:].rearrange("p a b -> p (a b)"), jau[:].rearrange("p a b -> p (a b)"),
                        reads=[jaub], writes=[jafb])
                for s_ in range(2):
                    kb.tt(oh[:], jaf[:, s_, :].unsqueeze(2).to_broadcast([128, 16, 16]),
                          iota.unsqueeze(1).to_broadcast([128, 16, 16]), ALU.is_equal,
                          reads=[jafb, csb], writes=[ohb])
                    kb.tt(oh[:], oh[:], idxf[:, 2 * h + s_, :].unsqueeze(1).to_broadcast([128, 16, 16]), ALU.mult,
                          reads=[ohb, idxfb], writes=[ohb])
                    kb.red(isel[:, s_, :], oh[:], ALU.add, reads=[ohb], writes=[iselb])
                kb.stt(ef[:], isel[:, 0, :], 128.0, isel[:, 1, :], ALU.mult, ALU.add, reads=[iselb], writes=[efb])
                e_i, e_ib = ei[h]
                kb.copy(e_i[:], ef[:], reads=[efb], writes=[e_ib])
                kb.ts(nmx[:, h:h + 1], sc16[:, 0:1], -1.0, None, ALU.mult, reads=[sc16b], writes=[nmxb])
                kb.act(exa[:, h, :], sc16[:], AF.Exp, reads=[sc16b, nmxb], writes=[exab, rsab], bias=nmx[:, h:h + 1],
                       scale=1.0, accum_out=rsa[:, h:h + 1])
            kb.op("dve", lambda: nc.vector.reciprocal(out=rsa[:], in_=rsa[:]), reads=[rsab], writes=[rsab])
            gi = 0
            for h in range(8):
                e_i, e_ib = ei[h]
                for k in range(16):
                    u, ubb = ub[gi % len(ub)]
                    gi += 1
                    kb.dma("pool", None, None, reads=[e_ib], writes=[ubb],
                           fn=lambda: nc.gpsimd.indirect_dma_start(
                               out=u[:], out_offset=None, in_=u_tab,
                               in_offset=bass.IndirectOffsetOnAxis(ap=e_i[:, k:k + 1], axis=0)))
                    kb.stt(junk[:], u[:], 1.0, xb_[:], ALU.mult, ALU.mult, reads=[ubb, xbb], writes=[junkb, dotsb],
                           accum_out=dots[:, k:k + 1])
                w_, w_b = wg[h % 2]
                kb.act(w_[:], dots[:], AF.Gelu, reads=[dotsb], writes=[w_b])
                kb.tt(w_[:], w_[:], exa[:, h, :], ALU.mult, reads=[w_b, exab], writes=[w_b])
                kb.ts(w_[:], w_[:], rsa[:, h:h + 1], None, ALU.mult, reads=[w_b, rsab], writes=[w_b])
                for k in range(16):
                    v, vbb = vb[gi % len(vb)]
                    gi += 1
                    kb.dma("pool", None, None, reads=[e_ib], writes=[vbb],
                           fn=lambda: nc.gpsimd.indirect_dma_start(
                               out=v[:], out_offset=None, in_=v_tab,
                               in_offset=bass.IndirectOffsetOnAxis(ap=e_i[:, k:k + 1], axis=0)))
                    kb.stt(acc[:], v[:], w_[:, k:k + 1], acc[:], ALU.mult, ALU.add, reads=[vbb, w_b, accb], writes=[accb])
            kb.tt(acc[:], acc[:], mod[:, 3072:4096], ALU.mult, reads=[accb, modb], writes=[accb])
            kb.stt(acc[:], x1[:], ALPHA, acc[:], ALU.mult, ALU.add, reads=[x1b, accb], writes=[accb])
            emit_ln(kb, tt_, ttb, acc, accb, ln[:, 2, :], lnb, ln[:, 3, :], lnb, st6, st6b, mv, mvb)
            kb.dma("sp", out[t], tt_[:], reads=[ttb], writes=[obuf], nowaw=True)
        kb.finish([obuf])
    return nc


SEQ = 8192
CTX = 256
BATCH = 4
NT_TOK = 33
NKT_ALL = 66
NCT_ALL = 2
_PROGS = {}


def _prog(key, fn):
    if key not in _PROGS:
        _PROGS[key] = fn()
    return _PROGS[key]


def _run(nc, in_maps):
    import time as _t
    t0 = _t.time()
    in_maps = [{k: np.ascontiguousarray(v, dtype=np.float32) for k, v in m.items()} for m in in_maps]
    nb = sum(v.nbytes for m in in_maps for v in m.values())
    res = run_bass_kernel_spmd(nc, in_maps, core_ids=list(range(len(in_maps))))
    print("[launch] in_bytes=%.1fMB  %.1fs" % (nb / 1e6, _t.time() - t0), flush=True)
    return res.results


def _rope_tables(dim):
    n_freq = dim // 4
    inv = 10000.0 ** (-np.arange(n_freq, dtype=np.float64) / n_freq)
    pos = np.arange(SEQ)
    row = (pos // 64).astype(np.float64)
    colp = (pos % 64).astype(np.float64)
    ang = np.concatenate([row[:, None] * inv, colp[:, None] * inv], axis=-1)
    ang = ang.astype(np.float32).astype(np.float64)
    return np.concatenate([np.cos(ang), np.sin(ang)], -1).astype(np.float32)


def _tok_tiles(xl, xc, core):
    b, half = core // 2, core % 2
    lat = xl[b, half * 4096:(half + 1) * 4096].reshape(32, 128, -1)
    ctx = xc[b, half * 128:(half + 1) * 128].reshape(1, 128, -1)
    return np.concatenate([lat, ctx], 0)


def _untile(outs, width):
    lat = np.zeros((BATCH, SEQ, width), np.float32)
    ctx = np.zeros((BATCH, CTX, width), np.float32)
    for core in range(NCORES):
        b, half = core // 2, core % 2
        o = outs[core]
        lat[b, half * 4096:(half + 1) * 4096] = o[:32].reshape(4096, width)
        ctx[b, half * 128:(half + 1) * 128] = o[32]
    return lat, ctx


def _colT(v):
    return np.ascontiguousarray(v.reshape(8, 128).T)


def kernel(x, c, ctx, c_ctx, ada_w, ada_b, ln1_g, ln1_b, ln2_g, ln2_b,
           ar_w_in, mla_q_norm, mla_w_uq, mla_kv_norm, mla_w_ukv, ret_gn_g, ar_w_out,
           dn_w_in, dn_conv, dn_a_log, dn_dt_bias, dn_norm_g, dn_w_out,
           peer_w_q, peer_k1, peer_k2, peer_u, peer_v):
    f32 = np.float32
    xl = np.asarray(x, f32)
    xc = np.asarray(ctx, f32)
    ident = np.eye(128, dtype=f32)
    ropem_full = _rope_tables(32)
    roper_full = _rope_tables(64)
    one0 = lambda n: np.concatenate([np.ones((128, n), f32), np.zeros((128, n), f32)], -1)
    cstC = np.zeros((128, 144), f32)
    cstC[:, :128] = ident
    cstC[:, 128:] = np.arange(16)[None]
    NK = NKT_ALL * 128
    for l in range(DEPTH):
        even = (l % 2 == 0)
        j = l // 2
        ncA = _prog(("A", even), lambda: build_A(even, NT_TOK, 32))
        maps = []
        for core in range(NCORES):
            b, half = core // 2, core % 2
            m = dict(x=_tok_tiles(xl, xc, core), cvl=_colT(np.asarray(c[b], f32)), cvc=_colT(np.asarray(c_ctx, f32)),
                     ada_w=ada_w[l][:, 0:2048], ada_b=ada_b[l][0:2048], ident=ident)
            if even:
                sl = slice(half * 4096, (half + 1) * 4096)
                m.update(w_in=ar_w_in[j], q_norm=mla_q_norm[j], w_uq=mla_w_uq[j], kv_norm=mla_kv_norm[j],
                         w_ukv=mla_w_ukv[j],
                         ropem=np.concatenate([ropem_full[sl].reshape(32, 128, 32), one0(16)[None]], 0),
                         roper=np.concatenate([roper_full[sl].reshape(32, 128, 64), one0(32)[None]], 0))
            else:
                m.update(w_in=dn_w_in[j], a_log=np.asarray(dn_a_log[j], f32).reshape(16),
                         dt_bias=np.asarray(dn_dt_bias[j], f32).reshape(16))
            maps.append(m)
        resA = _run(ncA, maps)
        W = A_EVEN_W if even else A_ODD_W
        zl, zc = _untile([r["out"] for r in resA], W)
        seq = np.concatenate([zc, zl], 1)
        del resA
        if even:
            ncM = _prog("ME", lambda: build_ME(NKT_ALL, NCT_ALL, 4))
            SEL = np.zeros((65, 64), f32)
            SEL[64] = 1
            pos = np.arange(128, dtype=np.float64)
            maps = []
            for core in range(NCORES):
                b, hh = core // 2, core % 2
                heads = np.arange(4) + hh * 4
                s = seq[b]
                q = s[:, 0:768].reshape(NK, 8, 96)
                kv = s[:, 768:1792].reshape(NK, 8, 128)
                kr = s[:, 1792:1824]
                QT = np.zeros((4, 128, NK), f32)
                KT = np.zeros((4, 128, NK), f32)
                VA = np.ones((4, 128, NKT_ALL, 65), f32)
                KN2 = np.zeros((4, 1, NK), f32)
                tokm = lambda a: a.reshape(NKT_ALL, 128, 64).transpose(1, 0, 2)
                rq = s[:, 1840:2352].reshape(NK, 8, 64)
                rk = s[:, 2352:2864].reshape(NK, 8, 64)
                rv = s[:, 2864:3376].reshape(NK, 8, 64)
                RQT = np.zeros((4, 64, NK), f32)
                RKT = np.zeros((4, 64, NK), f32)
                RK = np.zeros((4, 128, NKT_ALL, 64), f32)
                RV = np.zeros((4, 128, NKT_ALL, 64), f32)
                for i, h in enumerate(heads):
                    QT[i, 32:128] = q[:, h].T
                    QT[i, 0] = s[:, 1824 + h]
                    KT[i, 32:96] = kv[:, h, :64].T
                    KT[i, 96:128] = kr.T
                    KT[i, 0] = 1.0
                    VA[i, :, :, :64] = tokm(kv[:, h, 64:])
                    KN2[i, 0] = s[:, 1832 + h]
                    RQT[i] = rq[:, h].T
                    RKT[i] = rk[:, h].T
                    RK[i] = tokm(rk[:, h])
                    RV[i] = tokm(rv[:, h])
                gam = 1.0 - 2.0 ** (-5.0 - heads.astype(np.float64))
                lg = np.log1p(-2.0 ** (-5.0 - heads.astype(np.float64)))
                DSYM = np.exp(lg[:, None, None] * np.abs(pos[:, None] - pos[None, :])[None]).astype(f32)
                DCOL = np.stack([np.exp(lg[None] * (127 - pos[:, None])), np.exp(lg[None] * pos[:, None]),
                                 np.exp(lg[None] * (pos[:, None] + 1)), np.exp(lg[None] * (128 - pos[:, None]))],
                                -1).astype(f32)
                GC = np.broadcast_to(np.exp(lg * 128)[None, :], (64, 4)).astype(f32)
                maps.append(dict(QT=QT, KT=KT, VA=VA, KN2=KN2, SEL=SEL, RQT=RQT, RKT=RKT, RK=RK, RV=RV,
                                 DSYM=DSYM, DCOL=DCOL, GC=GC))
            resM = _run(ncM, maps)
            mix = np.zeros((BATCH, NK, 1024), f32)
            for core in range(NCORES):
                b, hh = core // 2, core % 2
                r = resM[core]
                for i in range(4):
                    h = hh * 4 + i
                    mix[b, :, h * 64:(h + 1) * 64] = r["OT"][i].T
                    mix[b, :, 512 + h * 64:512 + (h + 1) * 64] = r["RO"][i].transpose(1, 0, 2).reshape(NK, 64)
            gate = seq[:, :, 3376:3888]
            del resM, maps
            mixc, mixl = mix[:, :CTX], mix[:, CTX:]
        else:
            ncM = _prog("MO", lambda: build_MO(NKT_ALL, NCT_ALL, 4))
            CM = mo_consts()
            maps = []
            for core in range(NCORES):
                b, hh = core // 2, core % 2
                s = seq[b]
                PZ = np.zeros((3, 4, 128, NK + 8), f32)
                CWt = np.zeros((4, 128, 3, 5), f32)
                GG = np.zeros((128, 2, NKT_ALL, 4), f32)
                BB = np.zeros((128, 2, NKT_ALL, 4), f32)
                for i in range(4):
                    h = hh * 4 + i
                    for w in range(3):
                        cols = slice(w * 1024 + h * 128, w * 1024 + (h + 1) * 128)
                        zz = s[:, cols].T
                        PZ[w, i, :, 2:2 + CTX] = zz[:, :CTX]
                        PZ[w, i, :, CTX + 6:CTX + 6 + SEQ] = zz[:, CTX:]
                        CWt[i, :, w, :] = np.asarray(dn_conv[j], f32)[:, cols].T
                    for d in range(2):
                        GG[:, d, :, i] = s[:, 4096 + d * 8 + h].reshape(NKT_ALL, 128).T
                        BB[:, d, :, i] = s[:, 4112 + d * 8 + h].reshape(NKT_ALL, 128).T
                maps.append(dict(PZ=PZ, CW=CWt, GG=GG, BB=BB, CM=CM))
            resM = _run(ncM, maps)
            mix2 = np.zeros((2, BATCH, NK, 1024), f32)
            for core in range(NCORES):
                b, hh = core // 2, core % 2
                o = resM[core]["OUT"]
                for i in range(4):
                    h = hh * 4 + i
                    for d in range(2):
                        mix2[d, b, :, h * 128:(h + 1) * 128] = o[d, i].transpose(1, 0, 2).reshape(NK, 128)
            gate = seq[:, :, 3072:4096]
            del resM, maps
        gatec, gatel = gate[:, :CTX], gate[:, CTX:]
        ncC = _prog(("C", even), lambda: build_C(even, NT_TOK, 32))
        kT = np.zeros((128, 16, 128), f32)
        for h in range(8):
            kT[:, 2 * h, :] = np.asarray(peer_k1[l][h], f32).T
            kT[:, 2 * h + 1, :] = np.asarray(peer_k2[l][h], f32).T
        lnp = np.stack([ln1_g[l], ln1_b[l], ln2_g[l], ln2_b[l]]).astype(f32)
        maps = []
        for core in range(NCORES):
            b, half = core // 2, core % 2
            m = dict(x=_tok_tiles(xl, xc, core), cvl=_colT(np.asarray(c[b], f32)), cvc=_colT(np.asarray(c_ctx, f32)),
                     ada_w=ada_w[l][:, 2048:6144], ada_b=ada_b[l][2048:6144], lnp=lnp, w_q=peer_w_q[l], kT=kT,
                     u_tab=peer_u[l], v_tab=peer_v[l], cst=cstC, gate=_tok_tiles(gatel, gatec, core))
            if even:
                m.update(mix=_tok_tiles(mixl, mixc, core), gn_g=ret_gn_g[j], w_out=ar_w_out[j])
            else:
                m.update(mix=np.stack([_tok_tiles(mix2[d][:, CTX:], mix2[d][:, :CTX], core) for d in range(2)]),
                         gn_g=dn_norm_g[j], w_out=dn_w_out[j])
            maps.append(m)
        resC = _run(ncC, maps)
        xl, xc = _untile([r["out"] for r in resC], 1024)
        del resC, maps
    return xl
```
